# Optimizing a Trainium2 kernel written in Bass

```python
import math
import jax, jax.numpy as jnp
from jax import lax
import numpy as np

D_MODEL = 2048
BATCH = 8
SEQ = 2048
DEPTH = 1

MIX_WIDTH = D_MODEL
SSM_WIDTH = MIX_WIDTH // 2
SSM_GROUP = 16
SSM_GROUPS = SSM_WIDTH // SSM_GROUP
SSM_STATE = 64
DT_MIN = 1e-3
DT_MAX = 1e-1
QK_NOPE_DIM = 128
QK_ROPE_DIM = 64
V_HEAD_DIM = 128
MLA_WIDTH = MIX_WIDTH - SSM_WIDTH
MLA_HEADS = MLA_WIDTH // V_HEAD_DIM
Q_LORA_RANK = D_MODEL // 4
KV_LORA_RANK = D_MODEL // 8
ROPE_THETA = 10000.0
Q_BLOCK = 128
IN_WIDTH = SSM_WIDTH + Q_LORA_RANK + KV_LORA_RANK + QK_ROPE_DIM
D_FF = ((8 * D_MODEL // 3 + 255) // 256) * 256
CONV_WIDTH = 3
RMS_EPS = 1e-6

kernel_name = 'hybrid_s5_mla_convffn_layer'


def _rmsnorm(x, w):
    xf = x.astype(jnp.float32)
    y = xf * lax.rsqrt(jnp.mean(xf * xf, axis=-1, keepdims=True) + RMS_EPS)
    return (y * w.astype(jnp.float32)).astype(x.dtype)


def _rope_tables(positions, dtype):
    inv_freq = ROPE_THETA ** (-jnp.arange(0, QK_ROPE_DIM, 2, dtype=jnp.float32) / QK_ROPE_DIM)
    ang = positions.astype(jnp.float32)[..., None] * inv_freq
    return jnp.cos(ang).astype(dtype), jnp.sin(ang).astype(dtype)


def _rope(x, cos, sin):
    x1, x2 = jnp.split(x, 2, axis=-1)
    return jnp.concatenate([x1 * cos - x2 * sin, x1 * sin + x2 * cos], axis=-1)


def _ssm_combine(left, right):
    ar1, ai1, br1, bi1 = left
    ar2, ai2, br2, bi2 = right
    ar = ar2 * ar1 - ai2 * ai1
    ai = ar2 * ai1 + ai2 * ar1
    br = ar2 * br1 - ai2 * bi1 + br2
    bi = ar2 * bi1 + ai2 * br1 + bi2
    return ar, ai, br, bi


def _s5_group(u, lam_re, lam_im, log_dt, b_re, b_im, c_re, c_im, d_skip, w_glu, b_glu):
    bsz, seq, _ = u.shape
    ug = u.astype(jnp.float32).reshape(bsz, seq, SSM_GROUPS, SSM_GROUP)
    lr = lam_re.astype(jnp.float32)
    li = lam_im.astype(jnp.float32)
    dt = jnp.exp(log_dt.astype(jnp.float32))[:, None]
    mag = jnp.exp(lr * dt)
    abar_re = mag * jnp.cos(li * dt)
    abar_im = mag * jnp.sin(li * dt)
    nr, ni = abar_re - 1.0, abar_im
    den = lr * lr + li * li
    zr = (nr * lr + ni * li) / den
    zi = (ni * lr - nr * li) / den
    bre = b_re.astype(jnp.float32)
    bim = b_im.astype(jnp.float32)
    bbar_re = zr[..., None] * bre - zi[..., None] * bim
    bbar_im = zr[..., None] * bim + zi[..., None] * bre
    bu_re = jnp.einsum('blgh,gph->lbgp', ug, bbar_re)
    bu_im = jnp.einsum('blgh,gph->lbgp', ug, bbar_im)
    a_re = jnp.broadcast_to(abar_re, (seq, 1, SSM_GROUPS, SSM_STATE))
    a_im = jnp.broadcast_to(abar_im, (seq, 1, SSM_GROUPS, SSM_STATE))
    _, _, s_re, s_im = lax.associative_scan(_ssm_combine, (a_re, a_im, bu_re, bu_im), axis=0)
    y = (jnp.einsum('lbgp,ghp->blgh', s_re, c_re.astype(jnp.float32))
         - jnp.einsum('lbgp,ghp->blgh', s_im, c_im.astype(jnp.float32))
         + d_skip.astype(jnp.float32).reshape(SSM_GROUPS, SSM_GROUP) * ug)
    y = jax.nn.gelu(y.reshape(bsz, seq, SSM_WIDTH)).astype(u.dtype)
    return y * jax.nn.sigmoid(y @ w_glu + b_glu)


def _mla_group(c_q, c_kv, k_pe, positions, q_norm_w, w_uq, kv_norm_w, w_ukv):
    bsz, seq, _ = c_q.shape
    q = (_rmsnorm(c_q, q_norm_w) @ w_uq).reshape(bsz, seq, MLA_HEADS, QK_NOPE_DIM + QK_ROPE_DIM)
    q_nope, q_pe = q[..., :QK_NOPE_DIM], q[..., QK_NOPE_DIM:]
    kv = (_rmsnorm(c_kv, kv_norm_w) @ w_ukv).reshape(bsz, seq, MLA_HEADS, QK_NOPE_DIM + V_HEAD_DIM)
    k_nope, v = kv[..., :QK_NOPE_DIM], kv[..., QK_NOPE_DIM:]
    cos, sin = _rope_tables(positions, q.dtype)
    q_pe = _rope(q_pe, cos[:, :, None, :], sin[:, :, None, :])
    k_pe = _rope(k_pe, cos, sin)
    scale = (QK_NOPE_DIM + QK_ROPE_DIM) ** -0.5
    neg = jnp.finfo(jnp.float32).min
    outs = []
    for blk in range(seq // Q_BLOCK):
        q0 = blk * Q_BLOCK
        kend = q0 + Q_BLOCK
        s = (jnp.einsum('bqhd,bkhd->bhqk', q_nope[:, q0:kend], k_nope[:, :kend])
             + jnp.einsum('bqhr,bkr->bhqk', q_pe[:, q0:kend], k_pe[:, :kend]))
        s = s.astype(jnp.float32) * scale
        causal = jnp.arange(kend)[None, :] <= (q0 + jnp.arange(Q_BLOCK))[:, None]
        s = jnp.where(causal, s, neg)
        p = jax.nn.softmax(s, axis=-1).astype(v.dtype)
        outs.append(jnp.einsum('bhqk,bkhd->bqhd', p, v[:, :kend]))
    o = jnp.concatenate(outs, axis=1)
    return o.reshape(bsz, seq, MLA_WIDTH)


def _conv_ffn(h, w_up, conv_w, conv_b, w_down):
    a = h @ w_up
    a = lax.conv_general_dilated(a, conv_w[:, None, :], window_strides=(1,),
                                 padding=[(CONV_WIDTH - 1, 0)],
                                 dimension_numbers=('NWC', 'WIO', 'NWC'),
                                 feature_group_count=2 * D_FF) + conv_b
    gate, val = jnp.split(a, 2, axis=-1)
    return (jax.nn.silu(gate) * val) @ w_down


def setup_inputs(seed: int = 0) -> dict:
    key = jax.random.key(seed)
    ks = jax.random.split(key, 32)
    f32 = jnp.float32

    def nrm(k, shape, scale):
        return jax.random.normal(k, (DEPTH,) + shape, f32) * scale

    def gain(k, n):
        return 1.0 + 0.02 * jax.random.normal(k, (DEPTH, n), f32)

    x = jax.random.normal(ks[0], (BATCH, SEQ, D_MODEL), f32)
    offs = jax.random.randint(ks[1], (BATCH, 1), 0, 1024, dtype=jnp.int32)
    positions = offs + jnp.arange(SEQ, dtype=jnp.int32)[None, :]
    lam_re = -0.5 + 0.01 * jax.random.normal(ks[4], (DEPTH, SSM_GROUPS, SSM_STATE), f32)
    lam_im = (math.pi * jnp.arange(SSM_STATE, dtype=f32))[None, None, :] + 0.01 * jax.random.normal(ks[5], (DEPTH, SSM_GROUPS, SSM_STATE), f32)
    log_dt = jax.random.uniform(ks[6], (DEPTH, SSM_GROUPS), f32, math.log(DT_MIN), math.log(DT_MAX))
    return {
        'x': x,
        'positions': positions,
        'attn_norm_w': gain(ks[2], D_MODEL),
        'w_in': nrm(ks[3], (D_MODEL, IN_WIDTH), D_MODEL ** -0.5),
        'ssm_lambda_re': lam_re,
        'ssm_lambda_im': lam_im,
        'ssm_log_dt': log_dt,
        'ssm_b_re': nrm(ks[7], (SSM_GROUPS, SSM_STATE, SSM_GROUP), (2 * SSM_GROUP) ** -0.5),
        'ssm_b_im': nrm(ks[8], (SSM_GROUPS, SSM_STATE, SSM_GROUP), (2 * SSM_GROUP) ** -0.5),
        'ssm_c_re': nrm(ks[9], (SSM_GROUPS, SSM_GROUP, SSM_STATE), (2 * SSM_STATE) ** -0.5),
        'ssm_c_im': nrm(ks[10], (SSM_GROUPS, SSM_GROUP, SSM_STATE), (2 * SSM_STATE) ** -0.5),
        'ssm_d': nrm(ks[11], (SSM_WIDTH,), 1.0),
        'ssm_w_glu': nrm(ks[12], (SSM_WIDTH, SSM_WIDTH), SSM_WIDTH ** -0.5),
        'ssm_b_glu': nrm(ks[13], (SSM_WIDTH,), 0.01),
        'mla_q_norm_w': gain(ks[14], Q_LORA_RANK),
        'mla_w_uq': nrm(ks[15], (Q_LORA_RANK, MLA_HEADS * (QK_NOPE_DIM + QK_ROPE_DIM)), Q_LORA_RANK ** -0.5),
        'mla_kv_norm_w': gain(ks[16], KV_LORA_RANK),
        'mla_w_ukv': nrm(ks[17], (KV_LORA_RANK, MLA_HEADS * (QK_NOPE_DIM + V_HEAD_DIM)), KV_LORA_RANK ** -0.5),
        'ssm_out_norm_w': gain(ks[18], SSM_WIDTH),
        'mla_out_norm_w': gain(ks[19], MLA_WIDTH),
        'w_out': nrm(ks[20], (MIX_WIDTH, D_MODEL), MIX_WIDTH ** -0.5),
        'ffn_norm_w': gain(ks[21], D_MODEL),
        'ffn_w_up': nrm(ks[22], (D_MODEL, 2 * D_FF), D_MODEL ** -0.5),
        'ffn_conv_w': nrm(ks[23], (CONV_WIDTH, 2 * D_FF), CONV_WIDTH ** -0.5),
        'ffn_conv_b': nrm(ks[24], (2 * D_FF,), 0.01),
        'ffn_w_down': nrm(ks[25], (D_FF, D_MODEL), D_FF ** -0.5),
        'final_norm_w': 1.0 + 0.02 * jax.random.normal(ks[26], (D_MODEL,), f32),
    }


def reference(x, positions, attn_norm_w, w_in, ssm_lambda_re, ssm_lambda_im, ssm_log_dt,
              ssm_b_re, ssm_b_im, ssm_c_re, ssm_c_im, ssm_d, ssm_w_glu, ssm_b_glu,
              mla_q_norm_w, mla_w_uq, mla_kv_norm_w, mla_w_ukv, ssm_out_norm_w,
              mla_out_norm_w, w_out, ffn_norm_w, ffn_w_up, ffn_conv_w, ffn_conv_b,
              ffn_w_down, final_norm_w):
    split_at = [SSM_WIDTH, SSM_WIDTH + Q_LORA_RANK, SSM_WIDTH + Q_LORA_RANK + KV_LORA_RANK]
    h = x
    for l in range(DEPTH):
        hn = _rmsnorm(h, attn_norm_w[l])
        proj = hn @ w_in[l]
        u, c_q, c_kv, k_pe = jnp.split(proj, split_at, axis=-1)
        y_ssm = _s5_group(u, ssm_lambda_re[l], ssm_lambda_im[l], ssm_log_dt[l],
                          ssm_b_re[l], ssm_b_im[l], ssm_c_re[l], ssm_c_im[l],
                          ssm_d[l], ssm_w_glu[l], ssm_b_glu[l])
        y_mla = _mla_group(c_q, c_kv, k_pe, positions, mla_q_norm_w[l], mla_w_uq[l],
                           mla_kv_norm_w[l], mla_w_ukv[l])
        y = jnp.concatenate([_rmsnorm(y_ssm, ssm_out_norm_w[l]),
                             _rmsnorm(y_mla, mla_out_norm_w[l])], axis=-1)
        h = h + y @ w_out[l]
        h = h + _conv_ffn(_rmsnorm(h, ffn_norm_w[l]), ffn_w_up[l], ffn_conv_w[l],
                          ffn_conv_b[l], ffn_w_down[l])
    return _rmsnorm(h, final_norm_w)
```

```python
import math
import os
import bisect
import numpy as np
import concourse.bass as bass
import concourse.mybir as mybir
from concourse.bass_utils import run_bass_kernel_spmd

F32 = mybir.dt.float32
BF16 = mybir.dt.bfloat16
I32 = mybir.dt.int32
AF = mybir.ActivationFunctionType
ALU = mybir.AluOpType
AX = mybir.AxisListType

D = 2048
L = 2048
NT = 16
DFF = 5632
EPS = 1e-6
BASE = 16512
KB = 1024
SCALE = 192.0 ** -0.5


class Prog:
    SIGCH = 30000

    def __init__(self, nc):
        self.nc = nc
        self.ops = []
        self.dma_cnt = {}

    def add(self, eng, fn, r=(), w=()):
        self.ops.append((eng, fn, tuple(r), tuple(w), None, 0))

    def dma(self, q, fn, semkey, r=(), w=()):
        c = self.dma_cnt.get(semkey, 0) + 16
        self.dma_cnt[semkey] = c
        self.ops.append((q, fn, tuple(r), tuple(w), semkey, c))

    def finalize(self):
        nc = self.nc
        ops = self.ops
        n = len(ops)
        last_w = {}
        readers = {}
        deps = []
        dma_hist = {}
        for i, (eng, fn, r, w, sk, cnt) in enumerate(ops):
            d = {}
            for k in r:
                j = last_w.get(k)
                if j is not None:
                    d[j] = 'raw'
            for k in w:
                j = last_w.get(k)
                if j is not None and j not in d:
                    d[j] = 'waw'
                for j in readers.get(k, ()):
                    if j != i and j not in d:
                        d[j] = 'war'
            deps.append(d)
            for k in r:
                readers.setdefault(k, []).append(i)
            for k in w:
                last_w[k] = i
                readers[k] = []
            if sk is not None:
                dma_hist.setdefault(sk, ([], []))
                dma_hist[sk][0].append(i)
                dma_hist[sk][1].append(cnt)
        needs = [False] * n
        for i in range(n):
            eng = ops[i][0]
            for j, kind in deps[i].items():
                if ops[j][4] is not None:
                    continue
                ej = ops[j][0]
                if ej == eng and eng == 'pe':
                    continue
                needs[j] = True
        sigidx = [0] * n
        cnt = {}
        for i in range(n):
            if needs[i]:
                e = ops[i][0]
                cnt[e] = cnt.get(e, 0) + 1
                sigidx[i] = cnt[e]
        engsem = {}
        for e, c in cnt.items():
            engsem[e] = [nc.alloc_semaphore("s_%s_%d" % (e, q)) for q in range((c + self.SIGCH - 1) // self.SIGCH)]
        dmasem = {sk: nc.alloc_semaphore("d_%s" % str(sk)) for sk in dma_hist}
        seen = {}
        waitlist = []
        for i in range(n):
            eng = ops[i][0]
            waits = {}
            for j, kind in deps[i].items():
                sk = ops[j][4]
                if sk is not None:
                    idxs, cnts = dma_hist[sk]
                    p = bisect.bisect_left(idxs, i) - 1
                    val = cnts[p]
                    sid = ('d', sk)
                    sem = dmasem[sk]
                else:
                    if not needs[j]:
                        continue
                    ej = ops[j][0]
                    if ej == eng and eng == 'pe':
                        continue
                    s = sigidx[j] - 1
                    ch = s // self.SIGCH
                    val = s % self.SIGCH + 1
                    sid = ('e', ej, ch)
                    sem = engsem[ej][ch]
                if val > waits.get(sid, (None, 0))[1]:
                    waits[sid] = (sem, val)
            wl = []
            for sid, (sem, val) in waits.items():
                if seen.get((eng, sid), 0) >= val:
                    continue
                seen[(eng, sid)] = val
                wl.append((sem, val))
            waitlist.append(wl)
        per = {}
        for i in range(n):
            per.setdefault(ops[i][0], []).append(i)
        self.stats = {k: len(v) for k, v in per.items()}

        def mk(ename):
            def body(e):
                for i in per.get(ename, ()):
                    for sem, val in waitlist[i]:
                        e.wait_ge(sem, val)
                    ins = ops[i][1](e)
                    sk = ops[i][4]
                    if sk is not None:
                        ins.then_inc(dmasem[sk], 16)
                    elif needs[i]:
                        s = sigidx[i] - 1
                        ins.then_inc(engsem[ename][s // self.SIGCH], 1)
            return body

        with nc.Block() as block:
            block.tensor(mk('pe'))
            block.scalar(mk('act'))
            block.vector(mk('dve'))
            block.gpsimd(mk('pool'))
            block.sync(mk('sp'))


def build_program(debug=False):
    nc = bass.Bass("TRN2", target_bir_lowering=False)
    P = Prog(nc)

    def din(name, shape, dt=F32):
        return nc.dram_tensor(name, list(shape), dt, kind="ExternalInput").ap()

    x = din("x", [L, D])
    pos = din("pos", [L, 1], I32)
    attn_norm_w = din("attn_norm_w", [D])
    w_in = din("w_in", [D, 1856])
    lam_re = din("lam_re", [64, 64])
    lam_im = din("lam_im", [64, 64])
    log_dt = din("log_dt", [1, 64])
    b_re = din("b_re", [64, 64, 16])
    b_im = din("b_im", [64, 64, 16])
    c_re = din("c_re", [64, 16, 64])
    c_im = din("c_im", [64, 16, 64])
    ssm_d = din("ssm_d", [1024])
    w_glu = din("w_glu", [1024, 1024])
    b_glu = din("b_glu", [1024])
    q_norm_w = din("q_norm_w", [512])
    w_uq = din("w_uq", [512, 1536])
    kv_norm_w = din("kv_norm_w", [256])
    w_ukv = din("w_ukv", [256, 2048])
    ssm_out_norm_w = din("ssm_out_norm_w", [1024])
    mla_out_norm_w = din("mla_out_norm_w", [1024])
    w_out = din("w_out", [D, D])
    ffn_norm_w = din("ffn_norm_w", [D])
    w_up = din("w_up", [D, 2 * DFF])
    conv_w = din("conv_w", [3, 2 * DFF])
    conv_b = din("conv_b", [2 * DFF])
    w_down = din("w_down", [DFF, D])
    final_norm_w = din("final_norm_w", [1, D])
    c_ident = din("c_ident", [128, 128])
    c_swap = din("c_swap", [128, 128])
    c_cmask = din("c_cmask", [128, 128])
    c_colmask = din("c_colmask", [128, 8 * 128])
    c_grpmask = din("c_grpmask", [128, 8])
    c_invf = din("c_invf", [128, 32])
    c_sgn = din("c_sgn", [128, 1])
    out = nc.dram_tensor("out", [L, D], F32, kind="ExternalOutput").ap()
    dbg = {}

    def dout(name, shape, dt=F32):
        t = nc.dram_tensor(name, list(shape), dt, kind="ExternalOutput").ap()
        dbg[name] = t
        return t

    def sb(name, shape, dt, off):
        return nc.alloc_sbuf_tensor_at(name, list(shape), dt, offset=BASE + off)

    cp = [0]

    def csb(name, shape, dt):
        nb = int(np.prod(shape[1:])) * (4 if dt in (F32, I32) else 2)
        off = cp[0]
        cp[0] = (off + nb + 31) // 32 * 32
        assert cp[0] <= 20 * KB, cp[0]
        return sb(name, shape, dt, off)

    ident_f = csb("ident_f", [128, 128], F32)
    swap_f = csb("swap_f", [128, 128], F32)
    cmask_f = csb("cmask_f", [128, 128], F32)
    ones_f = csb("ones_f", [128, 128], F32)
    ones_b = csb("ones_b", [128, 128], BF16)
    ident_b = csb("ident_b", [128, 128], BF16)
    colmask_f = None
    colmask_b = csb("colmask_b", [128, 8, 128], BF16)
    grpmask = csb("grpmask", [128, 8], F32)
    invf = csb("invf", [128, 32], F32)
    sgn = csb("sgn", [128, 1], F32)
    anw = csb("anw", [128, 16], F32)
    fnw = csb("fnw", [128, 16], F32)
    qnw = csb("qnw", [128, 4], F32)
    kvnw = csb("kvnw", [128, 2], F32)
    monw = csb("monw", [128, 8], F32)
    sonw = csb("sonw", [128, 8], F32)
    ssmd = csb("ssmd", [128, 8], F32)
    bglu = csb("bglu", [128, 8], F32)
    convw = csb("convw", [128, 3, 88], F32)
    convb = csb("convb", [128, 88], F32)
    tails = csb("tails", [128, 88, 2], F32)
    cos_t = csb("cos_t", [128, 16, 32], F32)
    sin_t = csb("sin_t", [128, 16, 32], F32)
    fnb = csb("fnb", [128, 2048], F32)
    st = csb("st", [128, 64], F32)
    epsc = csb("epsc", [128, 1], F32)
    halfpi = csb("halfpi", [128, 1], F32)
    posi = csb("posi", [128, 16], I32)
    posf = csb("posf", [128, 16], F32)
    rt = csb("rt", [128, 4, 32], F32)

    psS = nc.alloc_psum_tensor("psS", [128, 2048], F32)
    psA = nc.alloc_psum_tensor("psA", [128, 512], F32)
    psB = nc.alloc_psum_tensor("psB", [128, 512], F32)
    psT = [nc.alloc_psum_tensor("psT0", [128, 1024], BF16), nc.alloc_psum_tensor("psT1", [128, 1024], BF16)]

    def SQ(c):
        return psS[:, c * 512:(c + 1) * 512]

    AL = dict(allow_slow_non_contiguous=True)

    def cdma(dst, src, key, slow=False, r=()):
        sk = 'c2' if len(r) else 'c'
        if slow:
            P.dma('sp', lambda e: e.dma_start(out=dst, in_=src, **AL), sk, r=r, w=[key])
        else:
            P.dma('sp', lambda e: e.dma_start(out=dst, in_=src), sk, r=r, w=[key])

    def wdma(dst, src, semkey, key, r=()):
        P.dma('pool', lambda e: e.dma_start(out=dst, in_=src), semkey, r=r, w=[key])

    def mm(o, lhsT, rhs, start, stop, r, w):
        P.add('pe', lambda e: e.matmul(o, lhsT, rhs, start=start, stop=stop), r=r, w=w)

    def tr(o, in_, ident, r, w):
        P.add('pe', lambda e: e.transpose(o, in_, ident), r=r, w=w)

    def act(o, in_, func, r, w, bias=None, scale=None, accum=None):
        kw = {}
        if bias is not None:
            kw['bias'] = bias
        if scale is not None:
            kw['scale'] = scale
        if accum is not None:
            kw['accum_out'] = accum
        P.add('act', lambda e: e.activation(o, in_, func, **kw), r=r, w=w)

    def ts(eng, o, in0, s1, s2, op0, op1, r, w):
        if op1 is None:
            P.add(eng, lambda e: e.tensor_scalar(o, in0, s1, None, op0), r=r, w=w)
        else:
            P.add(eng, lambda e: e.tensor_scalar(o, in0, s1, s2, op0, op1), r=r, w=w)

    def tt(eng, o, in0, in1, op, r, w):
        P.add(eng, lambda e: e.tensor_tensor(o, in0, in1, op), r=r, w=w)

    def stt(o, in0, s, in1, op0, op1, r, w):
        P.add('dve', lambda e: e.scalar_tensor_tensor(o, in0, s, in1, op0, op1), r=r, w=w)

    def cpy(eng, o, in_, r, w):
        if eng == 'act':
            P.add('act', lambda e: e.copy(o, in_), r=r, w=w)
        else:
            P.add(eng, lambda e: e.tensor_copy(o, in_), r=r, w=w)

    def memset(eng, o, val, w):
        P.add(eng, lambda e: e.memset(o, val), w=w)

    evc = [0]

    def evac_scaled(o, in_, wcol, r, w):
        evc[0] += 1
        if evc[0] % 2 and os.environ.get('KACT', '0') == '1':
            P.add('act', lambda e: e.activation(o, in_, AF.Identity, scale=wcol), r=r, w=w)
        else:
            ts('dve', o, in_, wcol, None, ALU.mult, None, r, w)

    def evac(o, in_, r, w):
        evc[0] += 1
        cpy('act' if evc[0] % 2 else 'dve', o, in_, r, w)

    cdma(ident_f[:], c_ident, 'ident_f')
    cdma(swap_f[:], c_swap, 'swap_f')
    cdma(cmask_f[:], c_cmask, 'cmask_f')
    cdma(grpmask[:], c_grpmask, 'grpmask')
    cdma(invf[:], c_invf, 'invf')
    cdma(sgn[:], c_sgn, 'sgn')
    P.dma('pool', lambda e: e.dma_start(out=colmask_b[:].rearrange("p a b -> p (a b)"), in_=c_colmask), 'cc', w=['colmask_b'])
    P.dma('pool', lambda e: e.dma_start(out=ident_b[:], in_=c_ident), 'cc', w=['ident_b'])
    cdma(anw[:], attn_norm_w.rearrange("(k p) -> p k", p=128), 'anw', True)
    cdma(fnw[:], ffn_norm_w.rearrange("(k p) -> p k", p=128), 'fnw', True)
    cdma(qnw[:], q_norm_w.rearrange("(k p) -> p k", p=128), 'qnw', True)
    cdma(kvnw[:], kv_norm_w.rearrange("(k p) -> p k", p=128), 'kvnw', True)
    cdma(monw[:], mla_out_norm_w.rearrange("(k p) -> p k", p=128), 'monw', True)
    cdma(sonw[:], ssm_out_norm_w.rearrange("(k p) -> p k", p=128), 'sonw', True)
    cdma(ssmd[:], ssm_d.rearrange("(k p) -> p k", p=128), 'ssmd', True)
    cdma(bglu[:], b_glu.rearrange("(k p) -> p k", p=128), 'bglu', True)
    for j in range(3):
        cdma(convw[:, j, :], conv_w[j].rearrange("(k p) -> p k", p=128), ('convw', j), True)
    cdma(convb[:], conv_b.rearrange("(k p) -> p k", p=128), 'convb', True)
    cdma(fnb[:], final_norm_w.partition_broadcast(128), 'fnb')
    cdma(posi[:], pos.rearrange("(k p) o -> p (k o)", p=128), 'posi', True)
    memset('dve', ones_f[:], 1.0, ['ones_f'])
    memset('dve', ones_b[:], 1.0, ['ones_b'])
    memset('dve', epsc[:], EPS, ['epsc'])
    memset('dve', halfpi[:], math.pi / 2, ['halfpi'])
    memset('dve', tails[:], 0.0, ['tails'])

    STOP = int(os.environ.get('KSTOP', '99'))

    def fin():
        P.add('sp', lambda e: e.nop(), r=['dbgout', 'outdone'])
        return nc, P, dbg, out
    if STOP == 1:
        return fin()
    cpy('dve', posf[:], posi[:], ['posi'], ['posf'])
    TWO_PI = 2 * math.pi
    for i in range(NT):
        for which, tab in ((0, sin_t), (1, cos_t)):
            a, b_, c_, d_ = rt[:, 0, :], rt[:, 1, :], rt[:, 2, :], rt[:, 3, :]
            ts('dve', a, invf[:], posf[:, i:i + 1], 1.0 / TWO_PI, ALU.mult, ALU.mult, ['invf', 'posf'], ['rt0'])
            if which:
                ts('dve', a, a, 0.25, None, ALU.add, None, ['rt0'], ['rt0'])
            rti = rt[:, 1, :].bitcast(I32)
            cpy('dve', rti, a, ['rt0'], ['rt1'])
            cpy('dve', c_, rti, ['rt1'], ['rt2'])
            tt('dve', a, a, c_, ALU.subtract, ['rt0', 'rt2'], ['rt0'])
            ts('dve', c_, a, 0.5, None, ALU.is_gt, None, ['rt0'], ['rt2'])
            tt('dve', a, a, c_, ALU.subtract, ['rt0', 'rt2'], ['rt0'])
            ts('dve', c_, a, -0.5, None, ALU.is_lt, None, ['rt0'], ['rt2'])
            tt('dve', a, a, c_, ALU.add, ['rt0', 'rt2'], ['rt0'])
            act(tab[:, i, :], a, AF.Sin, ['rt0'], [('tab', which, i)], scale=TWO_PI)

    if STOP == 2:
        return fin()
    uT = sb("uT", [128, 8, 2048], BF16, 20 * KB)
    cqT = sb("cqT", [128, 4, 2048], BF16, 52 * KB)
    ckvT = sb("ckvT", [128, 2, 2048], BF16, 68 * KB)
    kpeT = sb("kpeT", [128, 2048], BF16, 76 * KB)
    w_inS = sb("w_inS", [128, 16, 1856], BF16, 80 * KB)
    hnT = sb("hnT", [128, 16, 512], BF16, 138 * KB)
    xb = [sb("xb0", [128, 2048], F32, 154 * KB), sb("xb1", [128, 2048], F32, 162 * KB)]
    hn = [sb("hn0", [128, 2048], BF16, 170 * KB), sb("hn1", [128, 2048], BF16, 174 * KB)]
    cq_tm = sb("cq_tm", [128, 512], BF16, 178 * KB)
    ckv_tm = sb("ckv_tm", [128, 256], BF16, 179 * KB)
    kpe_tm = sb("kpe_tm", [128, 128], BF16, 180 * KB)
    memset('dve', kpe_tm[:, 64:128], 0.0, ['kpe_pad'])
    ropet = sb("ropet", [128, 4, 32], F32, 181 * KB)
    junkA = sb("junkA", [128, 512], F32, 182 * KB)

    for k in range(16):
        wdma(w_inS[:, k, :], w_in[k * 128:(k + 1) * 128, :], 'w_in', ('w_in', k))

    if STOP == 3:
        return fin()

    def rstd_from_ss(col, n, keys_r, key_w):
        act(st[:, col:col + 1], st[:, col:col + 1], AF.Sqrt, keys_r + ['epsc'], [key_w], bias=epsc[:], scale=1.0 / n)
        P.add('dve', lambda e: e.reciprocal(st[:, col:col + 1], st[:, col:col + 1]), r=[key_w], w=[key_w])

    def rope_tm(src_ap3, dst_ap3, i, nh, r, w):
        cb = cos_t[:, i, :].unsqueeze(1).broadcast_to([128, nh, 32])
        sbb = sin_t[:, i, :].unsqueeze(1).broadcast_to([128, nh, 32])
        x1 = src_ap3[:, :, 0:32]
        x2 = src_ap3[:, :, 32:64]
        rr = list(r) + [('tab', 0, i), ('tab', 1, i)]
        t = [ropeT_ref[0][:, q, 0:nh, :] for q in range(4)]
        tt('dve', t[0], x1, cb, ALU.mult, rr, ['ropeT0'])
        tt('dve', t[1], x2, sbb, ALU.mult, rr, ['ropeT1'])
        tt('dve', t[2], x1, sbb, ALU.mult, rr, ['ropeT2'])
        tt('dve', t[3], x2, cb, ALU.mult, rr, ['ropeT3'])
        tt('dve', dst_ap3[:, :, 0:32], t[0], t[1], ALU.subtract, ['ropeT0', 'ropeT1', 'ropeT2', 'ropeT3'], w)
        tt('dve', dst_ap3[:, :, 32:64], t[2], t[3], ALU.add, ['ropeT2', 'ropeT3'], w)

    ropeT_ref = [sb("ropeT", [128, 4, 8, 32], F32, 183 * KB)]
    tcount = [0]

    def transpose_to(src_tm, nk, dst_fn, wcol_fn, r, wkeys, rows=128):
        for k0 in range(0, nk, 4):
            tb = tcount[0] % 2
            tcount[0] += 1
            pk = ('T', tb)
            ks = list(range(k0, min(nk, k0 + 4)))
            for q, k in enumerate(ks):
                o = psT[tb][0:rows, q * 128:(q + 1) * 128]
                src = src_tm(k) if callable(src_tm) else src_tm[:, k * rows:(k + 1) * rows]
                tr(o, src, ident_b[:], list(r) + ['ident_b'], [pk])
            beng = 'act' if tb == 0 else 'dve'
            for q, k in enumerate(ks):
                o = psT[tb][0:rows, q * 128:(q + 1) * 128]
                wc = wcol_fn(k)
                if wc is None:
                    cpy(beng, dst_fn(k), o, [pk], [wkeys(k)])
                else:
                    evac_scaled(dst_fn(k), o, wc, [pk], [wkeys(k)])

    for G in range(4):
        for j in range(4):
            i = G * 4 + j
            b = i % 2
            P.dma('sp', lambda e, i=i, b=b: e.dma_start(out=xb[b][:], in_=x[i * 128:(i + 1) * 128, :]), 'x%d' % b, w=[('xb', b)])
            act(hn[b][:], xb[b][:], AF.Square, [('xb', b)], [('hn', b), 'st0'], accum=st[:, 0:1])
            if STOP == 4:
                return fin()
            rstd_from_ss(0, D, ['st0'], 'st0')
            if STOP == 5:
                return fin()
            ts('dve', hn[b][:], xb[b][:], st[:, 0:1], None, ALU.mult, None, [('xb', b), 'st0'], [('hn', b)])
            if STOP == 6:
                return fin()
            transpose_to(hn[b], 16, lambda k, j=j: hnT[:, k, j * 128:(j + 1) * 128], lambda k: anw[:, k:k + 1],
                         [('hn', b)], lambda k, j=j: ('hnT', k, j))
            if STOP == 7:
                return fin()
        hk = [('hnT', k, j) for k in range(16) for j in range(4)]
        for ct in range(8):
            pk = 'A' if ct % 2 == 0 else 'B'
            pt = psA if ct % 2 == 0 else psB
            for k in range(16):
                mm(pt[:], w_inS[:, k, ct * 128:(ct + 1) * 128], hnT[:, k, :], k == 0, k == 15,
                   [('w_in', k)] + (hk if k == 0 else []), [pk])
            evac(uT[:, ct, G * 512:(G + 1) * 512], pt[:], [pk], [('uT', ct, G)])
        if STOP == 8:
            return fin()
        for j in range(4):
            i = G * 4 + j
            for k in range(16):
                mm(SQ(0), hnT[:, k, j * 128:(j + 1) * 128], w_inS[:, k, 1024:1536], k == 0, k == 15,
                   [('w_in', k)] + (hk if k == 0 else []), ['S0'])
            for k in range(16):
                mm(SQ(1)[:, 0:320], hnT[:, k, j * 128:(j + 1) * 128], w_inS[:, k, 1536:1856], k == 0, k == 15,
                   [('w_in', k)], ['S1'])
            act(junkA[:], SQ(0), AF.Square, ['S0'], ['junkA', 'st1'], accum=st[:, 1:2])
            rstd_from_ss(1, 512, ['st1'], 'st1')
            ts('dve', cq_tm[:], SQ(0), st[:, 1:2], None, ALU.mult, None, ['S0', 'st1'], ['cq_tm'])
            transpose_to(cq_tm, 4, lambda k, i=i: cqT[:, k, i * 128:(i + 1) * 128], lambda k: qnw[:, k:k + 1],
                         ['cq_tm'], lambda k, i=i: ('cqT', k, i))
            act(junkA[:, 0:256], SQ(1)[:, 0:256], AF.Square, ['S1'], ['junkA', 'st2'], accum=st[:, 2:3])
            rstd_from_ss(2, 256, ['st2'], 'st2')
            ts('dve', ckv_tm[:], SQ(1)[:, 0:256], st[:, 2:3], None, ALU.mult, None, ['S1', 'st2'], ['ckv_tm'])
            transpose_to(ckv_tm, 2, lambda k, i=i: ckvT[:, k, i * 128:(i + 1) * 128], lambda k: kvnw[:, k:k + 1],
                         ['ckv_tm'], lambda k, i=i: ('ckvT', k, i))
            if STOP == 9:
                return fin()
            rope_tm(SQ(1)[:, 256:320].rearrange("p (h d) -> p h d", h=1), kpe_tm[:, 0:64].rearrange("p (h d) -> p h d", h=1),
                    i, 1, ['S1'], ['kpe_tm'])
            if STOP == 10:
                return fin()
            transpose_to(kpe_tm, 1, lambda k, i=i: kpeT[:, i * 128:(i + 1) * 128], lambda k: None,
                         ['kpe_tm', 'kpe_pad'], lambda k, i=i: ('kpeT', i))
            if STOP == 11:
                return fin()

        if STOP == 12:
            return fin()
    if STOP == 13:
        return fin()
    def dump(name, ap2d, shape, dt, rkeys):
        if not debug:
            return
        t = dout(name, shape, dt)
        P.dma('sp', lambda e: e.dma_start(out=t, in_=ap2d), 'dbg', r=rkeys, w=['dbgout'])

    pak = ([('uT', ct, G) for ct in range(8) for G in range(4)] + [('cqT', k, i) for k in range(4) for i in range(16)]
           + [('ckvT', k, i) for k in range(2) for i in range(16)] + [('kpeT', i) for i in range(16)])
    memset('dve', st[:, 38:39], 0.0, ['phaseA_done'])
    P.add('dve', lambda e: e.memset(st[:, 39:40], 0.0), r=pak + ['phaseA_done'], w=['phaseA_done'])
    PA = ['phaseA_done']
    tp = [80 * KB]

    def tsb(name, shape, dt):
        nb = int(np.prod(shape[1:])) * (4 if dt in (F32, I32) else 2)
        off = tp[0]
        tp[0] = (off + nb + 31) // 32 * 32
        assert tp[0] <= 112 * KB, tp[0]
        return sb(name, shape, dt, off)

    def vt(name):
        t = tsb(name, [128, 64], F32)
        return (t[:], name)

    def vop(o, a, b_, op):
        tt('dve', o[0], a[0], b_[0], op, [a[1], b_[1]], [o[1]])

    def vsc(o, a, c1, op=ALU.mult):
        ts('dve', o[0], a[0], c1, None, op, None, [a[1]], [o[1]])

    LR, LI, LDT, DTt = vt("LR"), vt("LI"), vt("LDT"), vt("DTt")
    for half in (0, 64):
        cdma(LR[0][half:half + 64, :], lam_re.rearrange("g p -> p g"), 'LR', True, PA)
        cdma(LI[0][half:half + 64, :], lam_im.rearrange("g p -> p g"), 'LI', True, PA)
    cdma(LDT[0], log_dt.partition_broadcast(128), 'LDT', False, PA)
    Bstack = tsb("Bstack", [128, 64, 16], F32)
    Bswap = tsb("Bswap", [128, 64, 16], F32)
    BmAll = tsb("BmAll", [128, 64, 16], F32)
    Bt1 = tsb("Bt1", [128, 64, 16], F32)
    cdma(Bstack[0:64], b_re.rearrange("g p h -> p g h"), 'Bstack', False, PA)
    cdma(Bstack[64:128], b_im.rearrange("g p h -> p g h"), 'Bstack', False, PA)
    cdma(Bswap[0:64], b_im.rearrange("g p h -> p g h"), 'Bswap', False, PA)
    cdma(Bswap[64:128], b_re.rearrange("g p h -> p g h"), 'Bswap', False, PA)
    Cnat = tsb("Cnat", [128, 8, 128], F32)
    cdma(Cnat[:, :, 0:64], c_re.rearrange("(j g) h p -> (g h) j p", j=8), 'Cnat', False, PA)
    cdma(Cnat[:, :, 64:128], c_im.rearrange("(j g) h p -> (g h) j p", j=8), 'Cnat', False, PA)
    act(DTt[0], LDT[0], AF.Exp, ['LDT'], ['DTt'])
    lrdt, lidt, mag, s32, c32 = vt("lrdt"), vt("lidt"), vt("mag"), vt("s32"), vt("c32")
    vop(lrdt, LR, DTt, ALU.mult)
    vop(lidt, LI, DTt, ALU.mult)
    act(mag[0], lrdt[0], AF.Exp, ['lrdt'], ['mag'], scale=1.0 / 32)
    act(s32[0], lidt[0], AF.Sin, ['lidt'], ['s32'], scale=1.0 / 32)
    act(c32[0], lidt[0], AF.Sin, ['lidt', 'halfpi'], ['c32'], scale=1.0 / 32, bias=halfpi[:])
    X1, X2, T1, T2, T3, T4 = vt("X1"), vt("X2"), vt("T1"), vt("T2"), vt("T3"), vt("T4")
    vop(X1, mag, c32, ALU.mult)
    vop(X2, mag, s32, ALU.mult)
    ts('dve', X2[0], X2[0], sgn[:, 0:1], None, ALU.mult, None, ['X2', 'sgn'], ['X2'])
    for _ in range(5):
        vop(T1, X1, X1, ALU.mult)
        vop(T2, X2, X2, ALU.mult)
        vop(T3, X1, X2, ALU.mult)
        vop(X1, T1, T2, ALU.subtract)
        vsc(X2, T3, 2.0)

    def cmul(o, xx, yy):
        vop(T1, xx[0], yy[0], ALU.mult)
        vop(T2, xx[1], yy[1], ALU.mult)
        vop(T3, xx[0], yy[1], ALU.mult)
        vop(T4, xx[1], yy[0], ALU.mult)
        vop(o[0], T1, T2, ALU.subtract)
        vop(o[1], T3, T4, ALU.add)

    PW1 = sb("PW1", [128, 25, 64], F32, 170 * KB)
    PW2 = sb("PW2", [128, 25, 64], F32, 170 * KB + 6400)
    PWI = {}
    tmpbase = [(vt("tb0a"), vt("tb0b")), (vt("tb1a"), vt("tb1b"))]
    base = (X1, X2)
    idx = 0
    for s_ in range(4):
        mmax = 7 if s_ < 3 else 3
        prev = None
        for m in range(1, 9 if s_ < 3 else 4):
            if m <= mmax:
                cur = ((PW1[:, idx, :], ('pw1', idx)), (PW2[:, idx, :], ('pw2', idx)))
                PWI[(s_, m)] = idx
                idx += 1
            else:
                cur = tmpbase[s_ % 2]
            if m == 1:
                cpy('dve', cur[0][0], base[0][0], [base[0][1]], [cur[0][1]])
                cpy('dve', cur[1][0], base[1][0], [base[1][1]], [cur[1][1]])
            else:
                cmul(cur, prev, base)
            prev = cur
        base = prev
    assert idx == 24
    if STOP == 18:
        return fin()
    nr, LI2, den, rden, zr, zis, Zs = vt("nr"), vt("LI2"), vt("den"), vt("rden"), vt("zr"), vt("zis"), vt("Zs")
    vsc(nr, X1, -1.0, ALU.add)
    ts('dve', LI2[0], LI[0], sgn[:, 0:1], None, ALU.mult, None, ['LI', 'sgn'], ['LI2'])
    vop(T1, LR, LR, ALU.mult)
    vop(T2, LI, LI, ALU.mult)
    vop(den, T1, T2, ALU.add)
    P.add('dve', lambda e: e.reciprocal(rden[0], den[0]), r=['den'], w=['rden'])
    vop(T1, nr, LR, ALU.mult)
    vop(T2, X2, LI2, ALU.mult)
    vop(T3, T1, T2, ALU.add)
    vop(zr, T3, rden, ALU.mult)
    vop(T1, X2, LR, ALU.mult)
    vop(T2, nr, LI2, ALU.mult)
    vop(T3, T1, T2, ALU.subtract)
    vop(zis, T3, rden, ALU.mult)
    vsc(Zs, zis, -1.0)
    zr_b = zr[0].unsqueeze(2).broadcast_to([128, 64, 16])
    Zs_b = Zs[0].unsqueeze(2).broadcast_to([128, 64, 16])
    tt('dve', Bt1[:], Bstack[:], zr_b, ALU.mult, ['Bstack', 'zr'], ['Bt1'])
    tt('dve', BmAll[:], Bswap[:], Zs_b, ALU.mult, ['Bswap', 'Zs'], ['BmAll'])
    tt('dve', BmAll[:], BmAll[:], Bt1[:], ALU.add, ['BmAll', 'Bt1'], ['BmAll'])
    BmTt = sb("BmTt", [128, 8, 128], BF16, 183 * KB)
    CmStack = sb("CmStack", [128, 8, 128], BF16, 185 * KB)
    ts('dve', Cnat[:, :, 64:128], Cnat[:, :, 64:128], -1.0, None, ALU.mult, None, ['Cnat'], ['Cnat'])
    for j in range(8):
        pk, pt = ('A', psA) if j % 2 == 0 else ('B', psB)
        tr(pt[:, 0:128], BmAll[:, 8 * j:8 * j + 8, :].rearrange("p g h -> p (g h)"), ident_f[:], ['BmAll', 'ident_f'], [pk])
        evac(BmTt[:, j, :], pt[:, 0:128], [pk], [('BmTt', j)])
    for j in range(8):
        pk, pt = ('A', psA) if j % 2 == 0 else ('B', psB)
        tr(pt[:, 0:128], Cnat[:, j, :], ident_f[:], ['Cnat', 'ident_f'], [pk])
        evac(CmStack[:, j, :], pt[:, 0:128], [pk], [('CmStack', j)])

    if STOP == 19:
        return fin()
    yg = sb("yg", [128, 8, 2048], BF16, 80 * KB)
    PADS = [16, 64, 512, 0, 0]
    Bb = []
    off = 112 * KB
    for par in range(2):
        row = []
        for s_ in range(5):
            row.append(sb("Bst%d_%d" % (par, s_), [128, PADS[s_] + 2048], BF16, off))
            off += (PADS[s_] + 2048) * 2
        Bb.append(row)
    assert off <= 156 * KB
    MT = [sb("MT0", [128, 24, 128], BF16, 156 * KB), sb("MT1", [128, 24, 128], BF16, 156 * KB + 6144)]
    BmTg = [sb("BmTg0", [128, 128], BF16, 169 * KB), sb("BmTg1", [128, 128], BF16, 169 * KB + 256)]
    CmTg = [sb("CmTg0", [128, 128], BF16, 169 * KB + 512), sb("CmTg1", [128, 128], BF16, 169 * KB + 768)]
    yacc = sb("yacc", [128, 2048], F32, 187 * KB)
    gtmp = [sb("gtmp%d" % q, [128, 512], F32, 195 * KB + q * 2048) for q in range(2)]
    mtmp = [sb("mtmp%d" % q, [128, 128], F32, 199 * KB + q * 512) for q in range(4)]
    mtsw = sb("mtsw", [128, 24, 128], BF16, 201 * KB)
    for par in range(2):
        for s_ in range(3):
            P.add('pool', lambda e, par=par, s_=s_: e.memset(Bb[par][s_][:, 0:PADS[s_]], 0.0), r=PA, w=[('Bpad', par, s_)])
    mtc = [0]
    for g in range(64):
        par = g % 2
        j = g // 8
        gp = g % 8
        ts('dve', BmTg[par][:], BmTt[:, j, :], grpmask[:, gp:gp + 1], None, ALU.mult, None,
           [('BmTt', j), 'grpmask'], [('BmTg', par)])
        tt('pool', CmTg[par][:], CmStack[:, j, :], colmask_b[:, gp, :], ALU.mult, [('CmStack', j), 'colmask_b'], [('CmTg', par)])
        pw1k = [('pw1', i) for i in range(24)]
        pw2k = [('pw2', i) for i in range(24)]
        mtk = [('MT', par, i) for i in range(24)]
        tt('dve', MT[par][:], ident_f[:].unsqueeze(1).broadcast_to([128, 24, 128]),
           PW1[:, 0:24, g:g + 1].broadcast_to([128, 24, 128]), ALU.mult, ['ident_f'] + pw1k, mtk)
        tt('pool', mtsw[:], swap_f[:].unsqueeze(1).broadcast_to([128, 24, 128]),
           PW2[:, 0:24, g:g + 1].broadcast_to([128, 24, 128]), ALU.mult, ['swap_f'] + pw2k, ['mtsw'])
        tt('dve', MT[par][:], MT[par][:], mtsw[:], ALU.add, mtk + ['mtsw'], mtk)
        for ct in range(4):
            mm(SQ(ct), BmTg[par][:], uT[:, j, ct * 512:(ct + 1) * 512], True, True,
               [('BmTg', par)] + [('uT', j, G) for G in [ct]], ['S%d' % ct])
            evac(Bb[par][0][:, PADS[0] + ct * 512:PADS[0] + (ct + 1) * 512], SQ(ct), ['S%d' % ct], [('B', par, 0, ct)])
        for s_ in range(4):
            src = Bb[par][s_]
            dst = Bb[par][s_ + 1]
            pad = PADS[s_]
            padn = PADS[s_ + 1]
            step = 8 ** s_
            for ct in range(4):
                terms = [0] + list(range(1, 8 if s_ < 3 else min(3, ct) + 1))
                for ti, m in enumerate(terms):
                    lhsT = ident_b[:] if m == 0 else MT[par][:, PWI[(s_, m)], :]
                    o0 = pad + ct * 512 - m * step
                    rk = ['ident_b'] if m == 0 else [('MT', par, PWI[(s_, m)])]
                    if s_ < 3:
                        rk += [('B', par, s_, ct)]
                        rk += [('B', par, s_, ct - 1)] if ct > 0 else [('Bpad', par, s_)]
                    else:
                        rk += [('B', par, s_, ct - m)]
                    mm(SQ(ct), lhsT, src[:, o0:o0 + 512], ti == 0, ti == len(terms) - 1, rk, ['S%d' % ct])
                evac(dst[:, padn + ct * 512:padn + (ct + 1) * 512], SQ(ct), ['S%d' % ct], [('B', par, s_ + 1, ct)])
        for ct in range(4):
            pk, pt = ('A', psA) if ct % 2 == 0 else ('B', psB)
            mm(pt[:], CmTg[par][:], Bb[par][4][:, ct * 512:(ct + 1) * 512], True, True,
               [('CmTg', par), ('B', par, 4, ct)], [pk])
            ya = yacc[:, ct * 512:(ct + 1) * 512]
            if gp == 0:
                cpy('dve', ya, pt[:], [pk], [('yacc', ct)])
            else:
                tt('dve', ya, pt[:], ya, ALU.add, [pk, ('yacc', ct)], [('yacc', ct)])
        if gp == 7:
            for ct in range(4):
                ya = yacc[:, ct * 512:(ct + 1) * 512]
                cs = slice(ct * 512, (ct + 1) * 512)
                stt(ya, uT[:, j, cs], ssmd[:, j:j + 1], ya, ALU.mult, ALU.add, [('uT', j, ct), 'ssmd', ('yacc', ct)], [('yacc', ct)])
                tt('pool', gtmp[0][:], ya, ya, ALU.mult, [('yacc', ct)], ['gtmp0'])
                ts('pool', gtmp[0][:], gtmp[0][:], 0.044715, 1.0, ALU.mult, ALU.add, ['gtmp0'], ['gtmp0'])
                tt('pool', gtmp[0][:], gtmp[0][:], ya, ALU.mult, ['gtmp0', ('yacc', ct)], ['gtmp0'])
                act(gtmp[1][:], gtmp[0][:], AF.Sigmoid, ['gtmp0'], ['gtmp1'], scale=1.5957691216057308)
                tt('dve', yg[:, j, cs], ya, gtmp[1][:], ALU.mult, [('yacc', ct), 'gtmp1'], [('yg', j, ct)])
    if STOP == 20:
        for j in range(8):
            dump("d_yg%d" % j, yg[:, j, :], [128, 2048], BF16, [('yg', j, ct) for ct in range(4)])
        return fin()

    ysT = sb("ysT", [128, 8, 2048], BF16, 112 * KB)
    w_gluS = sb("w_gluS", [128, 8, 1024], BF16, 144 * KB)
    ys_tmp = sb("ys_tmp", [128, 8, 512], F32, 160 * KB)
    sqt = [sb("sqt%d" % q, [128, 512], BF16, 176 * KB + q * 2048) for q in range(2)]
    rstdt = sb("rstdt", [128, 512], F32, 180 * KB)
    sigt = [sb("sigt%d" % q, [128, 512], F32, 182 * KB + q * 2048) for q in range(2)]
    ygk = [('yg', j, ct) for j in range(8) for ct in range(4)]
    memset('dve', st[:, 40:41], 0.0, ['phaseB_done'])
    P.add('dve', lambda e: e.memset(st[:, 41:42], 0.0), r=ygk + ['phaseB_done'], w=['phaseB_done'])
    for k in range(8):
        wdma(w_gluS[:, k, :], w_glu[k * 128:(k + 1) * 128, :], 'w_glu', ('w_glu', k), r=['phaseB_done'])
    cc = [0]
    for ct in range(4):
        cs = slice(ct * 512, (ct + 1) * 512)
        for oc in range(8):
            pk, pt = ('A', psA) if cc[0] % 2 == 0 else ('B', psB)
            q = cc[0] % 2
            cc[0] += 1
            for k in range(8):
                mm(pt[:], w_gluS[:, k, oc * 128:(oc + 1) * 128], yg[:, k, cs], k == 0, k == 7,
                   [('w_glu', k), ('yg', k, ct), 'phaseB_done'], [pk])
            act(sigt[q][:], pt[:], AF.Sigmoid, [pk, 'bglu'], [('sigt', q)], bias=bglu[:, oc:oc + 1])
            tt('dve', ys_tmp[:, oc, :], yg[:, oc, cs], sigt[q][:], ALU.mult, [('yg', oc, ct), ('sigt', q)], [('ys_tmp', oc)])
            tt('pool', sqt[q][:], ys_tmp[:, oc, :], ys_tmp[:, oc, :], ALU.mult, [('ys_tmp', oc)], [('sqt', q)])
            mm(SQ(0), ones_b[:], sqt[q][:], oc == 0, oc == 7, ['ones_b', ('sqt', q)], ['S0'])
        act(rstdt[:], SQ(0), AF.Sqrt, ['S0', 'epsc'], ['rstdt'], bias=epsc[:], scale=1.0 / 1024)
        P.add('dve', lambda e: e.reciprocal(rstdt[:], rstdt[:]), r=['rstdt'], w=['rstdt'])
        for oc in range(8):
            stt(ysT[:, oc, cs], ys_tmp[:, oc, :], sonw[:, oc:oc + 1], rstdt[:], ALU.mult, ALU.mult,
                [('ys_tmp', oc), 'sonw', 'rstdt'], [('ysT', oc, ct)])
    if STOP == 21:
        for j in range(8):
            dump("d_ysT%d" % j, ysT[:, j, :], [128, 2048], BF16, [('ysT', j, ct) for ct in range(4)])
        return fin()

    ystk = [('ysT', oc, ct) for oc in range(8) for ct in range(4)]
    memset('dve', st[:, 42:43], 0.0, ['phaseC_done'])
    P.add('dve', lambda e: e.memset(st[:, 43:44], 0.0), r=ystk + ['phaseC_done'], w=['phaseC_done'])
    KT = sb("KT", [128, 8, 2048], BF16, 20 * KB)
    Vt = sb("Vt", [128, 16, 1024], BF16, 80 * KB)
    w_ukvS = sb("w_ukvS", [128, 2, 2048], BF16, 144 * KB)
    for kk in range(2):
        wdma(w_ukvS[:, kk, :], w_ukv[kk * 128:(kk + 1) * 128, :], 'w_ukv', ('w_ukv', kk), r=['phaseC_done'])
    for G in range(4):
        for h in range(8):
            pk, pt = ('A', psA) if h % 2 == 0 else ('B', psB)
            for kk in range(2):
                mm(pt[:], w_ukvS[:, kk, h * 256:h * 256 + 128], ckvT[:, kk, G * 512:(G + 1) * 512], kk == 0, kk == 1,
                   [('w_ukv', kk), 'phaseC_done'] + [('ckvT', kk, G * 4 + jj) for jj in range(4)], [pk])
            evac(KT[:, h, G * 512:(G + 1) * 512], pt[:], [pk], [('KT', h, G)])
    for i in range(16):
        for half in range(2):
            pk, pt = ('A', psA) if half == 0 else ('B', psB)
            for hh in range(4):
                hd = half * 4 + hh
                for kk in range(2):
                    mm(pt[:, hh * 128:(hh + 1) * 128], ckvT[:, kk, i * 128:(i + 1) * 128],
                       w_ukvS[:, kk, hd * 256 + 128:hd * 256 + 256], kk == 0, kk == 1,
                       [('w_ukv', kk), ('ckvT', kk, i), 'phaseC_done'], [pk])
            evac(Vt[:, i, half * 512:(half + 1) * 512], pt[:], [pk], [('V', i, half)])

    if STOP == 23:
        return fin()
    ymT = sb("ymT", [128, 8, 2048], BF16, 144 * KB)
    w_uqS = sb("w_uqS", [128, 4, 1536], BF16, 176 * KB)
    Pb = sb("Pb", [128, 2048], BF16, 68 * KB)
    ropeT_ref[0] = sb("ropeT_D", [128, 4, 8, 32], F32, 72 * KB)
    q_tm = sb("q_tm", [128, 1600], BF16, 188 * KB)
    qTn = sb("qTn", [128, 8, 128], BF16, 188 * KB + 3200)
    qTp = sb("qTp", [128, 8, 128], BF16, 188 * KB + 5248)
    PT = sb("PT", [128, 16, 128], BF16, 188 * KB + 7296)
    o_tm = sb("o_tm", [128, 1024], F32, 188 * KB + 11392)
    on_tm = sb("on_tm", [128, 1024], BF16, 188 * KB + 15488)
    st2 = sb("st2", [128, 32], F32, 188 * KB + 17536)
    kvk = [('KT', h, G) for h in range(8) for G in range(4)] + [('V', i, hf) for i in range(16) for hf in range(2)]
    memset('dve', st[:, 44:45], 0.0, ['phaseA2_done'])
    P.add('dve', lambda e: e.memset(st[:, 45:46], 0.0), r=kvk + ['phaseA2_done'], w=['phaseA2_done'])
    for kq in range(4):
        wdma(w_uqS[:, kq, :], w_uq[kq * 128:(kq + 1) * 128, :], 'w_uq', ('w_uq', kq), r=['phaseA2_done'])
    P.add('dve', lambda e: e.memset(q_tm[:, 1536:1600], 0.0), r=['phaseA2_done'], w=['qTp_pad'])
    P.add('dve', lambda e: e.memset(kpeT[64:128, :], 0.0), r=['phaseA2_done'], w=['kpeT_pad'])
    if STOP == 30:
        return fin()
    for i in range(16):
        kend = (i + 1) * 128
        ncb = (kend + 511) // 512
        for c in range(3):
            for kq in range(4):
                mm(SQ(c), cqT[:, kq, i * 128:(i + 1) * 128], w_uqS[:, kq, c * 512:(c + 1) * 512], kq == 0, kq == 3,
                   [('cqT', kq, i), ('w_uq', kq), 'phaseA2_done'], ['S%d' % c])
        q3 = psS[:, 0:1536].rearrange("p (h d) -> p h d", h=8)
        qt3 = q_tm[:, 0:1536].rearrange("p (h d) -> p h d", h=8)
        for c in range(3):
            evac(q_tm[:, c * 512:(c + 1) * 512], SQ(c), ['S%d' % c], [('q_tm_c', c)])
        P.add('dve', lambda e: e.memset(st2[:, 31:32], 0.0), r=[('q_tm_c', c) for c in range(3)], w=['q_tm_n', 'q_tm_p'])
        if STOP == 31:
            return fin()
        rope_tm(qt3[:, :, 128:192], qt3[:, :, 128:192], i, 8, ['q_tm_p'], ['q_tm_p'])
        if STOP == 32:
            return fin()
        transpose_to(lambda k: q_tm[:, k * 192:k * 192 + 128], 8, lambda k: qTn[:, k, :], lambda k: None,
                     ['q_tm_n'], lambda k: ('qTn', k))
        if STOP == 33:
            return fin()
        KVAR = int(os.environ.get('KVAR', '0'))
        if KVAR == 1:
            transpose_to(lambda k: q_tm[:, k * 192:k * 192 + 128], 8, lambda k: qTp[:, k, :], lambda k: None,
                         ['q_tm_p', 'q_tm_n', 'qTp_pad'], lambda k: ('qTp', k))
        elif KVAR == 2:
            transpose_to(lambda k: q_tm[:, k * 192 + 128:k * 192 + 256], 4, lambda k: qTp[:, k, :], lambda k: None,
                         ['q_tm_p', 'q_tm_n', 'qTp_pad'], lambda k: ('qTp', k))
        else:
            transpose_to(lambda k: q_tm[:, k * 192 + 128:k * 192 + 256], 8, lambda k: qTp[:, k, :], lambda k: None,
                         ['q_tm_p', 'q_tm_n', 'qTp_pad'], lambda k: ('qTp', k))
        if STOP == 24:
            return fin()
        for h in range(8):
            sc = 8 + 4 * (h % 2)
            sk = ['S%d' % c for c in range(ncb)]
            for c in range(ncb):
                w_ = min(512, kend - c * 512)
                kr = [('KT', h, c)]
                mm(SQ(c)[:, 0:w_], qTn[:, h, :], KT[:, h, c * 512:c * 512 + w_], True, False,
                   [('qTn', h)] + kr + (['S0', 'S1', 'S2', 'q_tm_n'] if False else []), ['S%d' % c])
                mm(SQ(c)[:, 0:w_], qTp[:, h, :], kpeT[:, c * 512:c * 512 + w_], False, True,
                   [('qTp', h), 'qTp_pad', 'kpeT_pad'] + [('kpeT', ii) for ii in range(c * 4, min(16, c * 4 + 4))], ['S%d' % c])
            dc_, do_ = (kend - 128) // 512, (kend - 128) % 512
            tt('dve', SQ(dc_)[:, do_:do_ + 128], SQ(dc_)[:, do_:do_ + 128], cmask_f[:], ALU.add,
               ['S%d' % dc_, 'cmask_f'], ['S%d' % dc_])
            sc = 16 * (h % 2)
            for c in range(ncb):
                w_ = min(512, kend - c * 512)
                P.add('dve', lambda e, c=c, w_=w_, sc=sc: e.reduce_max(st2[:, sc + 4 + c:sc + 5 + c], SQ(c)[:, 0:w_], AX.X),
                      r=['S%d' % c], w=[('s2', sc + 4 + c)])
            P.add('dve', lambda e, sc=sc, ncb=ncb: e.reduce_max(st2[:, sc:sc + 1], st2[:, sc + 4:sc + 4 + ncb], AX.X),
                  r=[('s2', sc + 4 + c) for c in range(ncb)], w=[('s2', sc)])
            ts('dve', st2[:, sc + 1:sc + 2], st2[:, sc:sc + 1], -SCALE, None, ALU.mult, None, [('s2', sc)], [('s2', sc + 1)])
            for c in range(ncb):
                w_ = min(512, kend - c * 512)
                act(Pb[:, c * 512:c * 512 + w_], SQ(c)[:, 0:w_], AF.Exp, ['S%d' % c, ('s2', sc + 1)], ['Pb', ('s2', sc + 8 + c)],
                    bias=st2[:, sc + 1:sc + 2], scale=SCALE, accum=st2[:, sc + 8 + c:sc + 9 + c])
            P.add('dve', lambda e, sc=sc, ncb=ncb: e.reduce_sum(st2[:, sc + 2:sc + 3], st2[:, sc + 8:sc + 8 + ncb], AX.X),
                  r=[('s2', sc + 8 + c) for c in range(ncb)], w=[('s2', sc + 2)])
            P.add('dve', lambda e, sc=sc: e.reciprocal(st2[:, sc + 3:sc + 4], st2[:, sc + 2:sc + 3]),
                  r=[('s2', sc + 2)], w=[('s2', sc + 3)])
            if STOP == 25:
                return fin()
            transpose_to(lambda k: Pb[:, k * 128:(k + 1) * 128], i + 1, lambda k: PT[:, k, :], lambda k: None,
                         ['Pb'], lambda k: ('PT', k))
            pk, pt = ('A', psA) if h % 2 == 0 else ('B', psB)
            for blk in range(i + 1):
                mm(pt[:, 0:128], PT[:, blk, :], Vt[:, blk, h * 128:(h + 1) * 128], blk == 0, blk == i,
                   [('PT', blk), ('V', blk, h // 4)], [pk])
            ts('dve', o_tm[:, h * 128:(h + 1) * 128], pt[:, 0:128], st2[:, sc + 3:sc + 4], None, ALU.mult, None,
               [pk, ('s2', sc + 3)], [('o_tm', h)])
        if STOP == 26:
            return fin()
        otk = [('o_tm', h) for h in range(8)]
        act(on_tm[:], o_tm[:], AF.Square, otk, ['on_tm', 'st16'], accum=st[:, 16:17])
        rstd_from_ss(16, 1024, ['st16'], 'st16')
        ts('dve', on_tm[:], o_tm[:], st[:, 16:17], None, ALU.mult, None, otk + ['st16'], ['on_tm'])
        transpose_to(on_tm, 8, lambda k, i=i: ymT[:, k, i * 128:(i + 1) * 128], lambda k: monw[:, k:k + 1],
                     ['on_tm'], lambda k, i=i: ('ymT', k, i))
    if STOP == 22:
        allk = [('ymT', k, i) for k in range(8) for i in range(16)]
        srcs = [ysT[:, 0, :], ysT[:, 7, :], ymT[:, 0, :], ymT[:, 7, :], KT[:, 0, :], cqT[:, 0, :], kpeT[:, :], Vt[:, 0, :], Vt[:, 15, :]]
        for n_, sap in enumerate(srcs):
            wd_ = 2048 if n_ < 7 else 1024
            P.dma('pool', lambda e, n_=n_, sap=sap, wd_=wd_: e.dma_start(out=out[n_ * 128:(n_ + 1) * 128, 0:wd_], in_=sap), 'dmp',
                  r=allk, w=[('dmp', n_)])
        P.add('sp', lambda e: e.nop(), r=[('dmp', n_) for n_ in range(len(srcs))])
        return nc, P, dbg, out

    ymk = [('ymT', k, i) for k in range(8) for i in range(16)]
    memset('dve', st[:, 46:47], 0.0, ['phaseD1_done'])
    P.add('dve', lambda e: e.memset(st[:, 47:48], 0.0), r=ymk + ['phaseD1_done'], w=['phaseD1_done'])
    hT = sb("hT", [128, 4, 2048], F32, 20 * KB)
    hn2T = sb("hn2T", [128, 16, 512], BF16, 52 * KB)
    gb = [sb("gb%d" % q, [128, 4, 512], BF16, 68 * KB + q * 4096) for q in range(2)]
    wbufs = [sb("wb0", [128, 16, 512], BF16, 76 * KB), sb("wb1", [128, 16, 512], BF16, 92 * KB),
             sb("wb2", [128, 16, 512], BF16, 176 * KB)]
    hn2_tm = sb("hn2_tm", [128, 2048], BF16, 192 * KB)
    tg = [sb("tg%d" % q, [128, 512], F32, 196 * KB + q * 2048) for q in range(2)]
    sgt = sb("sgt", [128, 512], F32, 200 * KB)
    wbc = [0]

    hwc = [0]

    def next_wb(src_ap, view4=False):
        n = wbc[0] % 3
        wbc[0] += 1
        P.dma('pool', lambda e: e.dma_start(out=wbufs[n][:], in_=src_ap), 'wo%d' % n, r=['phaseD1_done'],
              w=[('wb', 2 * n), ('wb', 2 * n + 1)])
        return n

    def conv_evac(ps, pk, cidx, G, dst, dkey):
        w0, w1, w2 = convw[:, 0, cidx:cidx + 1], convw[:, 1, cidx:cidx + 1], convw[:, 2, cidx:cidx + 1]
        tk = ('tails', cidx)
        cw = [('convw', 0), ('convw', 1), ('convw', 2), 'convb']
        ts('dve', dst[:], ps, w2, convb[:, cidx:cidx + 1], ALU.mult, ALU.add, [pk] + cw, [dkey])
        stt(dst[:, 1:512], ps[:, 0:511], w1, dst[:, 1:512], ALU.mult, ALU.add, [pk, dkey] + cw, [dkey])
        stt(dst[:, 2:512], ps[:, 0:510], w0, dst[:, 2:512], ALU.mult, ALU.add, [pk, dkey] + cw, [dkey])
        if G > 0:
            stt(dst[:, 0:1], tails[:, cidx, 1:2], w1, dst[:, 0:1], ALU.mult, ALU.add, [tk, dkey] + cw, [dkey])
            stt(dst[:, 0:2], tails[:, cidx, 0:2], w0, dst[:, 0:2], ALU.mult, ALU.add, [tk, dkey] + cw, [dkey])
        if G < 3:
            cpy('dve', tails[:, cidx, 0:2], ps[:, 510:512], [pk], [tk])

    outk = []
    for G in range(4):
        for j in range(4):
            i = G * 4 + j
            P.dma('sp', lambda e, i=i, j=j: e.dma_start(out=hT[:, j, :], in_=x[i * 128:(i + 1) * 128, :]), 'h%d' % j,
                  r=['phaseD1_done'], w=[('h', j)])
        for dc in range(4):
            n = next_wb(w_out[:, dc * 512:(dc + 1) * 512].rearrange("(k p) n -> p k n", p=128))
            for j in range(4):
                i = G * 4 + j
                pk, pt = ('A', psA) if j % 2 == 0 else ('B', psB)
                for k in range(16):
                    src = ysT if k < 8 else ymT
                    rk = [('wb', 2 * n), ('wb', 2 * n + 1)]
                    rk += [('ysT', k, G)] if k < 8 else [('ymT', k - 8, i)]
                    mm(pt[:], src[:, k % 8, i * 128:(i + 1) * 128], wbufs[n][:, k, :], k == 0, k == 15,
                       rk + ['phaseD1_done'], [pk])
                hs = hT[:, j, dc * 512:(dc + 1) * 512]
                tt('dve', hs, pt[:], hs, ALU.add, [pk, ('h', j)], [('h', j)])
        for j in range(4):
            act(hn2_tm[:], hT[:, j, :], AF.Square, [('h', j)], ['hn2_tm', 'st20'], accum=st[:, 20:21])
            rstd_from_ss(20, D, ['st20'], 'st20')
            ts('dve', hn2_tm[:], hT[:, j, :], st[:, 20:21], None, ALU.mult, None, [('h', j), 'st20'], ['hn2_tm'])
            transpose_to(hn2_tm, 16, lambda k, j=j: hn2T[:, k, j * 128:(j + 1) * 128], lambda k: fnw[:, k:k + 1],
                         ['hn2_tm'], lambda k, j=j: ('hn2T', k, j))
        h2k = [('hn2T', k, j) for k in range(16) for j in range(4)]
        def hview(n, kind):
            flat = wbufs[n // 2][:].rearrange("p k c -> p (k c)")[:, (n % 2) * 4096:(n % 2 + 1) * 4096]
            if kind == 'up':
                return flat.rearrange("p (k c) -> p k c", k=16)
            return flat.rearrange("p (f c) -> p f c", f=2)

        def issue_half(hs):
            ns = []
            for kind, src in (('up', w_up[:, hs * 256:(hs + 1) * 256].rearrange("(k p) n -> p k n", p=128)),
                              ('up', w_up[:, DFF + hs * 256:DFF + (hs + 1) * 256].rearrange("(k p) n -> p k n", p=128)),
                              ('dn', w_down[hs * 256:(hs + 1) * 256, :].rearrange("(f p) n -> p f n", p=128))):
                n = hwc[0] % 6
                hwc[0] += 1
                wdma(hview(n, kind), src, 'wh%d' % n, ('wb', n), r=['phaseD1_done'])
                ns.append(n)
            return ns

        nxt = issue_half(0)
        for hs in range(22):
            ng, nv, nd = nxt
            if hs < 21:
                nxt = issue_half(hs + 1)
            gq = hs % 2
            wgv, wvv, wdv = hview(ng, 'up'), hview(nv, 'up'), hview(nd, 'dn')
            for ft in range(2):
                f = hs * 2 + ft
                cg, cv = 2 * (ft % 2), 2 * (ft % 2) + 1
                for k in range(16):
                    mm(SQ(cg), wgv[:, k, ft * 128:(ft + 1) * 128], hn2T[:, k, :], k == 0, k == 15,
                       [('wb', ng)] + (h2k if k == 0 else []), ['S%d' % cg])
                for k in range(16):
                    mm(SQ(cv), wvv[:, k, ft * 128:(ft + 1) * 128], hn2T[:, k, :], k == 0, k == 15,
                       [('wb', nv)] + (h2k if k == 0 else []), ['S%d' % cv])
                conv_evac(SQ(cg), 'S%d' % cg, f, G, tg[0], 'tg0')
                conv_evac(SQ(cv), 'S%d' % cv, 44 + f, G, tg[1], 'tg1')
                act(sgt[:], tg[0][:], AF.Silu, ['tg0'], ['sgt'])
                tt('pool', gb[gq][:, ft, :], sgt[:], tg[1][:], ALU.mult, ['sgt', 'tg1'], [('gb', gq, ft)])
            for j in range(4):
                for dc in range(4):
                    pk, pt = ('A', psA) if dc % 2 == 0 else ('B', psB)
                    for ft in range(2):
                        mm(pt[:], gb[gq][:, ft, j * 128:(j + 1) * 128], wdv[:, ft, dc * 512:(dc + 1) * 512],
                           ft == 0, ft == 1, [('gb', gq, ft), ('wb', nd)], [pk])
                    hs_ = hT[:, j, dc * 512:(dc + 1) * 512]
                    tt('dve', hs_, pt[:], hs_, ALU.add, [pk, ('h', j)], [('h', j)])
        for j in range(4):
            i = G * 4 + j
            act(hn2_tm[:], hT[:, j, :], AF.Square, [('h', j)], ['hn2_tm', 'st21'], accum=st[:, 21:22])
            rstd_from_ss(21, D, ['st21'], 'st21')
            stt(hT[:, j, :], hT[:, j, :], st[:, 21:22], fnb[:], ALU.mult, ALU.mult, [('h', j), 'st21', 'fnb'], [('h', j)])
            P.dma('sp', lambda e, i=i, j=j: e.dma_start(out=out[i * 128:(i + 1) * 128, :], in_=hT[:, j, :]), 'o%d' % j,
                  r=[('h', j)], w=[('out', i)])
            outk.append(('out', i))
    P.add('sp', lambda e: e.nop(), r=['dbgout'] + outk)
    return nc, P, dbg, out


def _consts():
    ident = np.eye(128, dtype=np.float32)
    swap = np.zeros((128, 128), np.float32)
    for q in range(128):
        swap[q, (q + 64) % 128] = 1.0
    qi = np.arange(128)[:, None]
    ki = np.arange(128)[None, :]
    cmask = np.where(ki <= qi, 0.0, -30000.0).astype(np.float32)
    colmask = np.zeros((128, 8, 128), np.float32)
    for g in range(8):
        colmask[:, g, g * 16:(g + 1) * 16] = 1.0
    grpmask = np.zeros((128, 8), np.float32)
    for g in range(8):
        grpmask[g * 16:(g + 1) * 16, g] = 1.0
    invf = (10000.0 ** (-np.arange(0, 64, 2, dtype=np.float32) / 64)).astype(np.float32)
    invf = np.broadcast_to(invf[None, :], (128, 32)).copy()
    sgn = np.ones((128, 1), np.float32)
    sgn[64:] = -1.0
    return dict(c_ident=ident, c_swap=swap, c_cmask=cmask, c_colmask=colmask.reshape(128, 1024),
                c_grpmask=grpmask, c_invf=invf, c_sgn=sgn)


def make_in_maps(inputs, cores):
    f = lambda a: np.ascontiguousarray(np.asarray(a))
    shared = dict(
        attn_norm_w=f(inputs['attn_norm_w'][0]), w_in=f(inputs['w_in'][0]),
        lam_re=f(inputs['ssm_lambda_re'][0]), lam_im=f(inputs['ssm_lambda_im'][0]),
        log_dt=f(inputs['ssm_log_dt'][0]).reshape(1, 64),
        b_re=f(inputs['ssm_b_re'][0]), b_im=f(inputs['ssm_b_im'][0]),
        c_re=f(inputs['ssm_c_re'][0]), c_im=f(inputs['ssm_c_im'][0]),
        ssm_d=f(inputs['ssm_d'][0]), w_glu=f(inputs['ssm_w_glu'][0]), b_glu=f(inputs['ssm_b_glu'][0]),
        q_norm_w=f(inputs['mla_q_norm_w'][0]), w_uq=f(inputs['mla_w_uq'][0]),
        kv_norm_w=f(inputs['mla_kv_norm_w'][0]), w_ukv=f(inputs['mla_w_ukv'][0]),
        ssm_out_norm_w=f(inputs['ssm_out_norm_w'][0]), mla_out_norm_w=f(inputs['mla_out_norm_w'][0]),
        w_out=f(inputs['w_out'][0]), ffn_norm_w=f(inputs['ffn_norm_w'][0]), w_up=f(inputs['ffn_w_up'][0]),
        conv_w=f(inputs['ffn_conv_w'][0]), conv_b=f(inputs['ffn_conv_b'][0]), w_down=f(inputs['ffn_w_down'][0]),
        final_norm_w=f(inputs['final_norm_w']).reshape(1, D),
    )
    shared.update(_consts())
    maps = []
    xs = np.asarray(inputs['x'])
    ps = np.asarray(inputs['positions'])
    for c in cores:
        m = dict(shared)
        m['x'] = f(xs[c])
        m['pos'] = f(ps[c]).reshape(L, 1).astype(np.int32)
        maps.append(m)
    return maps


def kernel(**inputs):
    nc, P, dbg, out = build_program(False)
    P.finalize()
    maps = make_in_maps(inputs, list(range(8)))
    res = run_bass_kernel_spmd(nc, maps, core_ids=list(range(8)))
    return np.stack([np.asarray(r["out"]) for r in res.results], axis=0).astype(np.float32)
```

```python
import math
import os
import bisect
import numpy as np
import concourse.bass as bass
import concourse.mybir as mybir
from concourse.bass_utils import run_bass_kernel_spmd

F32 = mybir.dt.float32
BF16 = mybir.dt.bfloat16
I32 = mybir.dt.int32
AF = mybir.ActivationFunctionType
ALU = mybir.AluOpType
AX = mybir.AxisListType

D = 2048
L = 2048
NT = 16
DFF = 5632
EPS = 1e-6
BASE = 16512
KB = 1024
SCALE = 192.0 ** -0.5


class Prog:
    SIGCH = 30000

    def __init__(self, nc):
        self.nc = nc
        self.ops = []
        self.dma_cnt = {}

    def add(self, eng, fn, r=(), w=()):
        self.ops.append((eng, fn, tuple(r), tuple(w), None, 0))

    def dma(self, q, fn, semkey, r=(), w=()):
        c = self.dma_cnt.get(semkey, 0) + 16
        self.dma_cnt[semkey] = c
        self.ops.append((q, fn, tuple(r), tuple(w), semkey, c))

    def finalize(self):
        nc = self.nc
        ops = self.ops
        n = len(ops)
        last_w = {}
        readers = {}
        deps = []
        dma_hist = {}
        for i, (eng, fn, r, w, sk, cnt) in enumerate(ops):
            d = {}
            for k in r:
                j = last_w.get(k)
                if j is not None:
                    d[j] = 'raw'
            for k in w:
                j = last_w.get(k)
                if j is not None and j not in d:
                    d[j] = 'waw'
                for j in readers.get(k, ()):
                    if j != i and j not in d:
                        d[j] = 'war'
            deps.append(d)
            for k in r:
                readers.setdefault(k, []).append(i)
            for k in w:
                last_w[k] = i
                readers[k] = []
            if sk is not None:
                dma_hist.setdefault(sk, ([], []))
                dma_hist[sk][0].append(i)
                dma_hist[sk][1].append(cnt)
        needs = [False] * n
        for i in range(n):
            eng = ops[i][0]
            for j, kind in deps[i].items():
                if ops[j][4] is not None:
                    continue
                ej = ops[j][0]
                if ej == eng and eng == 'pe':
                    continue
                needs[j] = True
        sigidx = [0] * n
        cnt = {}
        for i in range(n):
            if needs[i]:
                e = ops[i][0]
                cnt[e] = cnt.get(e, 0) + 1
                sigidx[i] = cnt[e]
        engsem = {}
        for e, c in cnt.items():
            engsem[e] = [nc.alloc_semaphore("s_%s_%d" % (e, q)) for q in range((c + self.SIGCH - 1) // self.SIGCH)]
        dmasem = {sk: nc.alloc_semaphore("d_%s" % str(sk)) for sk in dma_hist}
        seen = {}
        waitlist = []
        for i in range(n):
            eng = ops[i][0]
            waits = {}
            for j, kind in deps[i].items():
                sk = ops[j][4]
                if sk is not None:
                    idxs, cnts = dma_hist[sk]
                    p = bisect.bisect_left(idxs, i) - 1
                    val = cnts[p]
                    sid = ('d', sk)
                    sem = dmasem[sk]
                else:
                    if not needs[j]:
                        continue
                    ej = ops[j][0]
                    if ej == eng and eng == 'pe':
                        continue
                    s = sigidx[j] - 1
                    ch = s // self.SIGCH
                    val = s % self.SIGCH + 1
                    sid = ('e', ej, ch)
                    sem = engsem[ej][ch]
                if val > waits.get(sid, (None, 0))[1]:
                    waits[sid] = (sem, val)
            wl = []
            for sid, (sem, val) in waits.items():
                if seen.get((eng, sid), 0) >= val:
                    continue
                seen[(eng, sid)] = val
                wl.append((sem, val))
            waitlist.append(wl)
        per = {}
        for i in range(n):
            per.setdefault(ops[i][0], []).append(i)
        self.stats = {k: len(v) for k, v in per.items()}

        def mk(ename):
            def body(e):
                for i in per.get(ename, ()):
                    for sem, val in waitlist[i]:
                        e.wait_ge(sem, val)
                    ins = ops[i][1](e)
                    sk = ops[i][4]
                    if sk is not None:
                        ins.then_inc(dmasem[sk], 16)
                    elif needs[i]:
                        s = sigidx[i] - 1
                        ins.then_inc(engsem[ename][s // self.SIGCH], 1)
            return body

        with nc.Block() as block:
            block.tensor(mk('pe'))
            block.scalar(mk('act'))
            block.vector(mk('dve'))
            block.gpsimd(mk('pool'))
            block.sync(mk('sp'))


def build_program(debug=False):
    nc = bass.Bass("TRN2", target_bir_lowering=False)
    P = Prog(nc)

    def din(name, shape, dt=F32):
        return nc.dram_tensor(name, list(shape), dt, kind="ExternalInput").ap()

    x = din("x", [L, D])
    pos = din("pos", [L, 1], I32)
    attn_norm_w = din("attn_norm_w", [D])
    w_in = din("w_in", [D, 1856])
    lam_re = din("lam_re", [64, 64])
    lam_im = din("lam_im", [64, 64])
    log_dt = din("log_dt", [1, 64])
    b_re = din("b_re", [64, 64, 16])
    b_im = din("b_im", [64, 64, 16])
    c_re = din("c_re", [64, 16, 64])
    c_im = din("c_im", [64, 16, 64])
    ssm_d = din("ssm_d", [1024])
    w_glu = din("w_glu", [1024, 1024])
    b_glu = din("b_glu", [1024])
    q_norm_w = din("q_norm_w", [512])
    w_uq = din("w_uq", [512, 1536])
    kv_norm_w = din("kv_norm_w", [256])
    w_ukv = din("w_ukv", [256, 2048])
    ssm_out_norm_w = din("ssm_out_norm_w", [1024])
    mla_out_norm_w = din("mla_out_norm_w", [1024])
    w_out = din("w_out", [D, D])
    ffn_norm_w = din("ffn_norm_w", [D])
    w_up = din("w_up", [D, 2 * DFF])
    conv_w = din("conv_w", [3, 2 * DFF])
    conv_b = din("conv_b", [2 * DFF])
    w_down = din("w_down", [DFF, D])
    final_norm_w = din("final_norm_w", [1, D])
    c_ident = din("c_ident", [128, 128])
    c_swap = din("c_swap", [128, 128])
    c_cmask = din("c_cmask", [128, 128])
    c_colmask = din("c_colmask", [128, 8 * 128])
    c_grpmask = din("c_grpmask", [128, 8])
    c_invf = din("c_invf", [128, 32])
    c_sgn = din("c_sgn", [128, 1])
    out = nc.dram_tensor("out", [L, D], F32, kind="ExternalOutput").ap()
    dbg = {}

    def dout(name, shape, dt=F32):
        t = nc.dram_tensor(name, list(shape), dt, kind="ExternalOutput").ap()
        dbg[name] = t
        return t

    def sb(name, shape, dt, off):
        return nc.alloc_sbuf_tensor_at(name, list(shape), dt, offset=BASE + off)

    cp = [0]

    def csb(name, shape, dt):
        nb = int(np.prod(shape[1:])) * (4 if dt in (F32, I32) else 2)
        off = cp[0]
        cp[0] = (off + nb + 31) // 32 * 32
        assert cp[0] <= 20 * KB, cp[0]
        return sb(name, shape, dt, off)

    ident_f = csb("ident_f", [128, 128], F32)
    swap_f = csb("swap_f", [128, 128], F32)
    cmask_f = csb("cmask_f", [128, 128], F32)
    ones_f = csb("ones_f", [128, 128], F32)
    ones_b = csb("ones_b", [128, 128], BF16)
    ident_b = csb("ident_b", [128, 128], BF16)
    colmask_f = None
    colmask_b = csb("colmask_b", [128, 8, 128], BF16)
    grpmask = csb("grpmask", [128, 8], F32)
    invf = csb("invf", [128, 32], F32)
    sgn = csb("sgn", [128, 1], F32)
    anw = csb("anw", [128, 16], F32)
    fnw = csb("fnw", [128, 16], F32)
    qnw = csb("qnw", [128, 4], F32)
    kvnw = csb("kvnw", [128, 2], F32)
    monw = csb("monw", [128, 8], F32)
    sonw = csb("sonw", [128, 8], F32)
    ssmd = csb("ssmd", [128, 8], F32)
    bglu = csb("bglu", [128, 8], F32)
    convw = csb("convw", [128, 3, 88], F32)
    convb = csb("convb", [128, 88], F32)
    tails = csb("tails", [128, 88, 2], F32)
    cos_t = csb("cos_t", [128, 16, 32], F32)
    sin_t = csb("sin_t", [128, 16, 32], F32)
    fnb = csb("fnb", [128, 2048], F32)
    st = csb("st", [128, 64], F32)
    epsc = csb("epsc", [128, 1], F32)
    halfpi = csb("halfpi", [128, 1], F32)
    posi = csb("posi", [128, 16], I32)
    posf = csb("posf", [128, 16], F32)
    rt = csb("rt", [128, 4, 32], F32)

    psS = nc.alloc_psum_tensor("psS", [128, 2048], F32)
    psA = nc.alloc_psum_tensor("psA", [128, 512], F32)
    psB = nc.alloc_psum_tensor("psB", [128, 512], F32)
    psT = [nc.alloc_psum_tensor("psT0", [128, 1024], BF16), nc.alloc_psum_tensor("psT1", [128, 1024], BF16)]

    def SQ(c):
        return psS[:, c * 512:(c + 1) * 512]

    AL = dict(allow_slow_non_contiguous=True)

    def cdma(dst, src, key, slow=False, r=()):
        sk = 'c2' if len(r) else 'c'
        if slow:
            P.dma('sp', lambda e: e.dma_start(out=dst, in_=src, **AL), sk, r=r, w=[key])
        else:
            P.dma('sp', lambda e: e.dma_start(out=dst, in_=src), sk, r=r, w=[key])

    def wdma(dst, src, semkey, key, r=()):
        P.dma('pool', lambda e: e.dma_start(out=dst, in_=src), semkey, r=r, w=[key])

    def mm(o, lhsT, rhs, start, stop, r, w):
        P.add('pe', lambda e: e.matmul(o, lhsT, rhs, start=start, stop=stop), r=r, w=w)

    def tr(o, in_, ident, r, w):
        P.add('pe', lambda e: e.transpose(o, in_, ident), r=r, w=w)

    def act(o, in_, func, r, w, bias=None, scale=None, accum=None):
        kw = {}
        if bias is not None:
            kw['bias'] = bias
        if scale is not None:
            kw['scale'] = scale
        if accum is not None:
            kw['accum_out'] = accum
        P.add('act', lambda e: e.activation(o, in_, func, **kw), r=r, w=w)

    def ts(eng, o, in0, s1, s2, op0, op1, r, w):
        if op1 is None:
            P.add(eng, lambda e: e.tensor_scalar(o, in0, s1, None, op0), r=r, w=w)
        else:
            P.add(eng, lambda e: e.tensor_scalar(o, in0, s1, s2, op0, op1), r=r, w=w)

    def tt(eng, o, in0, in1, op, r, w):
        P.add(eng, lambda e: e.tensor_tensor(o, in0, in1, op), r=r, w=w)

    def stt(o, in0, s, in1, op0, op1, r, w):
        P.add('dve', lambda e: e.scalar_tensor_tensor(o, in0, s, in1, op0, op1), r=r, w=w)

    def cpy(eng, o, in_, r, w):
        if eng == 'act':
            P.add('act', lambda e: e.copy(o, in_), r=r, w=w)
        else:
            P.add(eng, lambda e: e.tensor_copy(o, in_), r=r, w=w)

    def memset(eng, o, val, w):
        P.add(eng, lambda e: e.memset(o, val), w=w)

    evc = [0]

    def evac_scaled(o, in_, wcol, r, w):
        evc[0] += 1
        if evc[0] % 2 and os.environ.get('KACT', '0') == '1':
            P.add('act', lambda e: e.activation(o, in_, AF.Identity, scale=wcol), r=r, w=w)
        else:
            ts('dve', o, in_, wcol, None, ALU.mult, None, r, w)

    def evac(o, in_, r, w):
        evc[0] += 1
        cpy('act' if evc[0] % 2 else 'dve', o, in_, r, w)

    cdma(ident_f[:], c_ident, 'ident_f')
    cdma(swap_f[:], c_swap, 'swap_f')
    cdma(cmask_f[:], c_cmask, 'cmask_f')
    cdma(grpmask[:], c_grpmask, 'grpmask')
    cdma(invf[:], c_invf, 'invf')
    cdma(sgn[:], c_sgn, 'sgn')
    P.dma('pool', lambda e: e.dma_start(out=colmask_b[:].rearrange("p a b -> p (a b)"), in_=c_colmask), 'cc', w=['colmask_b'])
    P.dma('pool', lambda e: e.dma_start(out=ident_b[:], in_=c_ident), 'cc', w=['ident_b'])
    cdma(anw[:], attn_norm_w.rearrange("(k p) -> p k", p=128), 'anw', True)
    cdma(fnw[:], ffn_norm_w.rearrange("(k p) -> p k", p=128), 'fnw', True)
    cdma(qnw[:], q_norm_w.rearrange("(k p) -> p k", p=128), 'qnw', True)
    cdma(kvnw[:], kv_norm_w.rearrange("(k p) -> p k", p=128), 'kvnw', True)
    cdma(monw[:], mla_out_norm_w.rearrange("(k p) -> p k", p=128), 'monw', True)
    cdma(sonw[:], ssm_out_norm_w.rearrange("(k p) -> p k", p=128), 'sonw', True)
    cdma(ssmd[:], ssm_d.rearrange("(k p) -> p k", p=128), 'ssmd', True)
    cdma(bglu[:], b_glu.rearrange("(k p) -> p k", p=128), 'bglu', True)
    for j in range(3):
        cdma(convw[:, j, :], conv_w[j].rearrange("(k p) -> p k", p=128), ('convw', j), True)
    cdma(convb[:], conv_b.rearrange("(k p) -> p k", p=128), 'convb', True)
    cdma(fnb[:], final_norm_w.partition_broadcast(128), 'fnb')
    cdma(posi[:], pos.rearrange("(k p) o -> p (k o)", p=128), 'posi', True)
    memset('dve', ones_f[:], 1.0, ['ones_f'])
    memset('dve', ones_b[:], 1.0, ['ones_b'])
    memset('dve', epsc[:], EPS, ['epsc'])
    memset('dve', halfpi[:], math.pi / 2, ['halfpi'])
    memset('dve', tails[:], 0.0, ['tails'])

    STOP = int(os.environ.get('KSTOP', '99'))

    def fin():
        P.add('sp', lambda e: e.nop(), r=['dbgout', 'outdone'])
        return nc, P, dbg, out
    if STOP == 1:
        return fin()
    cpy('dve', posf[:], posi[:], ['posi'], ['posf'])
    TWO_PI = 2 * math.pi
    for i in range(NT):
        for which, tab in ((0, sin_t), (1, cos_t)):
            a, b_, c_, d_ = rt[:, 0, :], rt[:, 1, :], rt[:, 2, :], rt[:, 3, :]
            ts('dve', a, invf[:], posf[:, i:i + 1], 1.0 / TWO_PI, ALU.mult, ALU.mult, ['invf', 'posf'], ['rt0'])
            if which:
                ts('dve', a, a, 0.25, None, ALU.add, None, ['rt0'], ['rt0'])
            rti = rt[:, 1, :].bitcast(I32)
            cpy('dve', rti, a, ['rt0'], ['rt1'])
            cpy('dve', c_, rti, ['rt1'], ['rt2'])
            tt('dve', a, a, c_, ALU.subtract, ['rt0', 'rt2'], ['rt0'])
            ts('dve', c_, a, 0.5, None, ALU.is_gt, None, ['rt0'], ['rt2'])
            tt('dve', a, a, c_, ALU.subtract, ['rt0', 'rt2'], ['rt0'])
            ts('dve', c_, a, -0.5, None, ALU.is_lt, None, ['rt0'], ['rt2'])
            tt('dve', a, a, c_, ALU.add, ['rt0', 'rt2'], ['rt0'])
            act(tab[:, i, :], a, AF.Sin, ['rt0'], [('tab', which, i)], scale=TWO_PI)

    if STOP == 2:
        return fin()
    uT = sb("uT", [128, 8, 2048], BF16, 20 * KB)
    cqT = sb("cqT", [128, 4, 2048], BF16, 52 * KB)
    ckvT = sb("ckvT", [128, 2, 2048], BF16, 68 * KB)
    kpeT = sb("kpeT", [128, 2048], BF16, 76 * KB)
    w_inS = sb("w_inS", [128, 16, 1856], BF16, 80 * KB)
    hnT = sb("hnT", [128, 16, 512], BF16, 138 * KB)
    xb = [sb("xb0", [128, 2048], F32, 154 * KB), sb("xb1", [128, 2048], F32, 162 * KB)]
    hn = [sb("hn0", [128, 2048], BF16, 170 * KB), sb("hn1", [128, 2048], BF16, 174 * KB)]
    cq_tm = sb("cq_tm", [128, 512], BF16, 178 * KB)
    ckv_tm = sb("ckv_tm", [128, 256], BF16, 179 * KB)
    kpe_tm = sb("kpe_tm", [128, 128], BF16, 180 * KB)
    memset('dve', kpe_tm[:, 64:128], 0.0, ['kpe_pad'])
    ropet = sb("ropet", [128, 4, 32], F32, 181 * KB)
    junkA = sb("junkA", [128, 512], F32, 182 * KB)

    for k in range(16):
        wdma(w_inS[:, k, :], w_in[k * 128:(k + 1) * 128, :], 'w_in', ('w_in', k))

    if STOP == 3:
        return fin()

    def rstd_from_ss(col, n, keys_r, key_w):
        act(st[:, col:col + 1], st[:, col:col + 1], AF.Sqrt, keys_r + ['epsc'], [key_w], bias=epsc[:], scale=1.0 / n)
        P.add('dve', lambda e: e.reciprocal(st[:, col:col + 1], st[:, col:col + 1]), r=[key_w], w=[key_w])

    def rope_tm(src_ap3, dst_ap3, i, nh, r, w):
        cb = cos_t[:, i, :].unsqueeze(1).broadcast_to([128, nh, 32])
        sbb = sin_t[:, i, :].unsqueeze(1).broadcast_to([128, nh, 32])
        x1 = src_ap3[:, :, 0:32]
        x2 = src_ap3[:, :, 32:64]
        rr = list(r) + [('tab', 0, i), ('tab', 1, i)]
        t = [ropeT_ref[0][:, q, 0:nh, :] for q in range(4)]
        tt('dve', t[0], x1, cb, ALU.mult, rr, ['ropeT0'])
        tt('dve', t[1], x2, sbb, ALU.mult, rr, ['ropeT1'])
        tt('dve', t[2], x1, sbb, ALU.mult, rr, ['ropeT2'])
        tt('dve', t[3], x2, cb, ALU.mult, rr, ['ropeT3'])
        tt('dve', dst_ap3[:, :, 0:32], t[0], t[1], ALU.subtract, ['ropeT0', 'ropeT1', 'ropeT2', 'ropeT3'], w)
        tt('dve', dst_ap3[:, :, 32:64], t[2], t[3], ALU.add, ['ropeT2', 'ropeT3'], w)

    ropeT_ref = [sb("ropeT", [128, 4, 8, 32], F32, 183 * KB)]
    tcount = [0]

    def transpose_to(src_tm, nk, dst_fn, wcol_fn, r, wkeys, rows=128):
        for k0 in range(0, nk, 4):
            tb = tcount[0] % 2
            tcount[0] += 1
            pk = ('T', tb)
            ks = list(range(k0, min(nk, k0 + 4)))
            for q, k in enumerate(ks):
                o = psT[tb][0:rows, q * 128:(q + 1) * 128]
                src = src_tm(k) if callable(src_tm) else src_tm[:, k * rows:(k + 1) * rows]
                tr(o, src, ident_b[:], list(r) + ['ident_b'], [pk])
            beng = 'act' if tb == 0 else 'dve'
            for q, k in enumerate(ks):
                o = psT[tb][0:rows, q * 128:(q + 1) * 128]
                wc = wcol_fn(k)
                if wc is None:
                    cpy(beng, dst_fn(k), o, [pk], [wkeys(k)])
                else:
                    evac_scaled(dst_fn(k), o, wc, [pk], [wkeys(k)])

    for G in range(4):
        for j in range(4):
            i = G * 4 + j
            b = i % 2
            P.dma('sp', lambda e, i=i, b=b: e.dma_start(out=xb[b][:], in_=x[i * 128:(i + 1) * 128, :]), 'x%d' % b, w=[('xb', b)])
            act(hn[b][:], xb[b][:], AF.Square, [('xb', b)], [('hn', b), 'st0'], accum=st[:, 0:1])
            if STOP == 4:
                return fin()
            rstd_from_ss(0, D, ['st0'], 'st0')
            if STOP == 5:
                return fin()
            ts('dve', hn[b][:], xb[b][:], st[:, 0:1], None, ALU.mult, None, [('xb', b), 'st0'], [('hn', b)])
            if STOP == 6:
                return fin()
            transpose_to(hn[b], 16, lambda k, j=j: hnT[:, k, j * 128:(j + 1) * 128], lambda k: anw[:, k:k + 1],
                         [('hn', b)], lambda k, j=j: ('hnT', k, j))
            if STOP == 7:
                return fin()
        hk = [('hnT', k, j) for k in range(16) for j in range(4)]
        for ct in range(8):
            pk = 'A' if ct % 2 == 0 else 'B'
            pt = psA if ct % 2 == 0 else psB
            for k in range(16):
                mm(pt[:], w_inS[:, k, ct * 128:(ct + 1) * 128], hnT[:, k, :], k == 0, k == 15,
                   [('w_in', k)] + (hk if k == 0 else []), [pk])
            evac(uT[:, ct, G * 512:(G + 1) * 512], pt[:], [pk], [('uT', ct, G)])
        if STOP == 8:
            return fin()
        for j in range(4):
            i = G * 4 + j
            for k in range(16):
                mm(SQ(0), hnT[:, k, j * 128:(j + 1) * 128], w_inS[:, k, 1024:1536], k == 0, k == 15,
                   [('w_in', k)] + (hk if k == 0 else []), ['S0'])
            for k in range(16):
                mm(SQ(1)[:, 0:320], hnT[:, k, j * 128:(j + 1) * 128], w_inS[:, k, 1536:1856], k == 0, k == 15,
                   [('w_in', k)], ['S1'])
            act(junkA[:], SQ(0), AF.Square, ['S0'], ['junkA', 'st1'], accum=st[:, 1:2])
            rstd_from_ss(1, 512, ['st1'], 'st1')
            ts('dve', cq_tm[:], SQ(0), st[:, 1:2], None, ALU.mult, None, ['S0', 'st1'], ['cq_tm'])
            transpose_to(cq_tm, 4, lambda k, i=i: cqT[:, k, i * 128:(i + 1) * 128], lambda k: qnw[:, k:k + 1],
                         ['cq_tm'], lambda k, i=i: ('cqT', k, i))
            act(junkA[:, 0:256], SQ(1)[:, 0:256], AF.Square, ['S1'], ['junkA', 'st2'], accum=st[:, 2:3])
            rstd_from_ss(2, 256, ['st2'], 'st2')
            ts('dve', ckv_tm[:], SQ(1)[:, 0:256], st[:, 2:3], None, ALU.mult, None, ['S1', 'st2'], ['ckv_tm'])
            transpose_to(ckv_tm, 2, lambda k, i=i: ckvT[:, k, i * 128:(i + 1) * 128], lambda k: kvnw[:, k:k + 1],
                         ['ckv_tm'], lambda k, i=i: ('ckvT', k, i))
            if STOP == 9:
                return fin()
            rope_tm(SQ(1)[:, 256:320].rearrange("p (h d) -> p h d", h=1), kpe_tm[:, 0:64].rearrange("p (h d) -> p h d", h=1),
                    i, 1, ['S1'], ['kpe_tm'])
            if STOP == 10:
                return fin()
            transpose_to(kpe_tm, 1, lambda k, i=i: kpeT[:, i * 128:(i + 1) * 128], lambda k: None,
                         ['kpe_tm', 'kpe_pad'], lambda k, i=i: ('kpeT', i))
            if STOP == 11:
                return fin()

        if STOP == 12:
            return fin()
    if STOP == 13:
        return fin()
    def dump(name, ap2d, shape, dt, rkeys):
        if not debug:
            return
        t = dout(name, shape, dt)
        P.dma('sp', lambda e: e.dma_start(out=t, in_=ap2d), 'dbg', r=rkeys, w=['dbgout'])

    pak = ([('uT', ct, G) for ct in range(8) for G in range(4)] + [('cqT', k, i) for k in range(4) for i in range(16)]
           + [('ckvT', k, i) for k in range(2) for i in range(16)] + [('kpeT', i) for i in range(16)])
    memset('dve', st[:, 38:39], 0.0, ['phaseA_done'])
    P.add('dve', lambda e: e.memset(st[:, 39:40], 0.0), r=pak + ['phaseA_done'], w=['phaseA_done'])
    PA = ['phaseA_done']
    tp = [80 * KB]

    def tsb(name, shape, dt):
        nb = int(np.prod(shape[1:])) * (4 if dt in (F32, I32) else 2)
        off = tp[0]
        tp[0] = (off + nb + 31) // 32 * 32
        assert tp[0] <= 112 * KB, tp[0]
        return sb(name, shape, dt, off)

    def vt(name):
        t = tsb(name, [128, 64], F32)
        return (t[:], name)

    def vop(o, a, b_, op):
        tt('dve', o[0], a[0], b_[0], op, [a[1], b_[1]], [o[1]])

    def vsc(o, a, c1, op=ALU.mult):
        ts('dve', o[0], a[0], c1, None, op, None, [a[1]], [o[1]])

    LR, LI, LDT, DTt = vt("LR"), vt("LI"), vt("LDT"), vt("DTt")
    for half in (0, 64):
        cdma(LR[0][half:half + 64, :], lam_re.rearrange("g p -> p g"), 'LR', True, PA)
        cdma(LI[0][half:half + 64, :], lam_im.rearrange("g p -> p g"), 'LI', True, PA)
    cdma(LDT[0], log_dt.partition_broadcast(128), 'LDT', False, PA)
    Bstack = tsb("Bstack", [128, 64, 16], F32)
    Bswap = tsb("Bswap", [128, 64, 16], F32)
    BmAll = tsb("BmAll", [128, 64, 16], F32)
    Bt1 = tsb("Bt1", [128, 64, 16], F32)
    cdma(Bstack[0:64], b_re.rearrange("g p h -> p g h"), 'Bstack', False, PA)
    cdma(Bstack[64:128], b_im.rearrange("g p h -> p g h"), 'Bstack', False, PA)
    cdma(Bswap[0:64], b_im.rearrange("g p h -> p g h"), 'Bswap', False, PA)
    cdma(Bswap[64:128], b_re.rearrange("g p h -> p g h"), 'Bswap', False, PA)
    Cnat = tsb("Cnat", [128, 8, 128], F32)
    cdma(Cnat[:, :, 0:64], c_re.rearrange("(j g) h p -> (g h) j p", j=8), 'Cnat', False, PA)
    cdma(Cnat[:, :, 64:128], c_im.rearrange("(j g) h p -> (g h) j p", j=8), 'Cnat', False, PA)
    act(DTt[0], LDT[0], AF.Exp, ['LDT'], ['DTt'])
    lrdt, lidt, mag, s32, c32 = vt("lrdt"), vt("lidt"), vt("mag"), vt("s32"), vt("c32")
    vop(lrdt, LR, DTt, ALU.mult)
    vop(lidt, LI, DTt, ALU.mult)
    act(mag[0], lrdt[0], AF.Exp, ['lrdt'], ['mag'], scale=1.0 / 32)
    act(s32[0], lidt[0], AF.Sin, ['lidt'], ['s32'], scale=1.0 / 32)
    act(c32[0], lidt[0], AF.Sin, ['lidt', 'halfpi'], ['c32'], scale=1.0 / 32, bias=halfpi[:])
    X1, X2, T1, T2, T3, T4 = vt("X1"), vt("X2"), vt("T1"), vt("T2"), vt("T3"), vt("T4")
    vop(X1, mag, c32, ALU.mult)
    vop(X2, mag, s32, ALU.mult)
    ts('dve', X2[0], X2[0], sgn[:, 0:1], None, ALU.mult, None, ['X2', 'sgn'], ['X2'])
    for _ in range(5):
        vop(T1, X1, X1, ALU.mult)
        vop(T2, X2, X2, ALU.mult)
        vop(T3, X1, X2, ALU.mult)
        vop(X1, T1, T2, ALU.subtract)
        vsc(X2, T3, 2.0)

    def cmul(o, xx, yy):
        vop(T1, xx[0], yy[0], ALU.mult)
        vop(T2, xx[1], yy[1], ALU.mult)
        vop(T3, xx[0], yy[1], ALU.mult)
        vop(T4, xx[1], yy[0], ALU.mult)
        vop(o[0], T1, T2, ALU.subtract)
        vop(o[1], T3, T4, ALU.add)

    PW1 = sb("PW1", [128, 25, 64], F32, 170 * KB)
    PW2 = sb("PW2", [128, 25, 64], F32, 170 * KB + 6400)
    PWI = {}
    tmpbase = [(vt("tb0a"), vt("tb0b")), (vt("tb1a"), vt("tb1b"))]
    base = (X1, X2)
    idx = 0
    for s_ in range(4):
        mmax = 7 if s_ < 3 else 3
        prev = None
        for m in range(1, 9 if s_ < 3 else 4):
            if m <= mmax:
                cur = ((PW1[:, idx, :], ('pw1', idx)), (PW2[:, idx, :], ('pw2', idx)))
                PWI[(s_, m)] = idx
                idx += 1
            else:
                cur = tmpbase[s_ % 2]
            if m == 1:
                cpy('dve', cur[0][0], base[0][0], [base[0][1]], [cur[0][1]])
                cpy('dve', cur[1][0], base[1][0], [base[1][1]], [cur[1][1]])
            else:
                cmul(cur, prev, base)
            prev = cur
        base = prev
    assert idx == 24
    if STOP == 18:
        return fin()
    nr, LI2, den, rden, zr, zis, Zs = vt("nr"), vt("LI2"), vt("den"), vt("rden"), vt("zr"), vt("zis"), vt("Zs")
    vsc(nr, X1, -1.0, ALU.add)
    ts('dve', LI2[0], LI[0], sgn[:, 0:1], None, ALU.mult, None, ['LI', 'sgn'], ['LI2'])
    vop(T1, LR, LR, ALU.mult)
    vop(T2, LI, LI, ALU.mult)
    vop(den, T1, T2, ALU.add)
    P.add('dve', lambda e: e.reciprocal(rden[0], den[0]), r=['den'], w=['rden'])
    vop(T1, nr, LR, ALU.mult)
    vop(T2, X2, LI2, ALU.mult)
    vop(T3, T1, T2, ALU.add)
    vop(zr, T3, rden, ALU.mult)
    vop(T1, X2, LR, ALU.mult)
    vop(T2, nr, LI2, ALU.mult)
    vop(T3, T1, T2, ALU.subtract)
    vop(zis, T3, rden, ALU.mult)
    vsc(Zs, zis, -1.0)
    zr_b = zr[0].unsqueeze(2).broadcast_to([128, 64, 16])
    Zs_b = Zs[0].unsqueeze(2).broadcast_to([128, 64, 16])
    tt('dve', Bt1[:], Bstack[:], zr_b, ALU.mult, ['Bstack', 'zr'], ['Bt1'])
    tt('dve', BmAll[:], Bswap[:], Zs_b, ALU.mult, ['Bswap', 'Zs'], ['BmAll'])
    tt('dve', BmAll[:], BmAll[:], Bt1[:], ALU.add, ['BmAll', 'Bt1'], ['BmAll'])
    BmTt = sb("BmTt", [128, 8, 128], BF16, 183 * KB)
    CmStack = sb("CmStack", [128, 8, 128], BF16, 185 * KB)
    ts('dve', Cnat[:, :, 64:128], Cnat[:, :, 64:128], -1.0, None, ALU.mult, None, ['Cnat'], ['Cnat'])
    for j in range(8):
        pk, pt = ('A', psA) if j % 2 == 0 else ('B', psB)
        tr(pt[:, 0:128], BmAll[:, 8 * j:8 * j + 8, :].rearrange("p g h -> p (g h)"), ident_f[:], ['BmAll', 'ident_f'], [pk])
        evac(BmTt[:, j, :], pt[:, 0:128], [pk], [('BmTt', j)])
    for j in range(8):
        pk, pt = ('A', psA) if j % 2 == 0 else ('B', psB)
        tr(pt[:, 0:128], Cnat[:, j, :], ident_f[:], ['Cnat', 'ident_f'], [pk])
        evac(CmStack[:, j, :], pt[:, 0:128], [pk], [('CmStack', j)])

    if STOP == 19:
        return fin()
    yg = sb("yg", [128, 8, 2048], BF16, 80 * KB)
    PADS = [16, 64, 512, 0, 0]
    Bb = []
    off = 112 * KB
    for par in range(2):
        row = []
        for s_ in range(5):
            row.append(sb("Bst%d_%d" % (par, s_), [128, PADS[s_] + 2048], BF16, off))
            off += (PADS[s_] + 2048) * 2
        Bb.append(row)
    assert off <= 156 * KB
    MT = [sb("MT0", [128, 24, 128], BF16, 156 * KB), sb("MT1", [128, 24, 128], BF16, 156 * KB + 6144)]
    BmTg = [sb("BmTg0", [128, 128], BF16, 169 * KB), sb("BmTg1", [128, 128], BF16, 169 * KB + 256)]
    CmTg = [sb("CmTg0", [128, 128], BF16, 169 * KB + 512), sb("CmTg1", [128, 128], BF16, 169 * KB + 768)]
    yacc = sb("yacc", [128, 2048], F32, 187 * KB)
    gtmp = [sb("gtmp%d" % q, [128, 512], F32, 195 * KB + q * 2048) for q in range(2)]
    mtmp = [sb("mtmp%d" % q, [128, 128], F32, 199 * KB + q * 512) for q in range(4)]
    mtsw = sb("mtsw", [128, 24, 128], BF16, 201 * KB)
    for par in range(2):
        for s_ in range(3):
            P.add('pool', lambda e, par=par, s_=s_: e.memset(Bb[par][s_][:, 0:PADS[s_]], 0.0), r=PA, w=[('Bpad', par, s_)])
    mtc = [0]
    for g in range(64):
        par = g % 2
        j = g // 8
        gp = g % 8
        ts('dve', BmTg[par][:], BmTt[:, j, :], grpmask[:, gp:gp + 1], None, ALU.mult, None,
           [('BmTt', j), 'grpmask'], [('BmTg', par)])
        tt('pool', CmTg[par][:], CmStack[:, j, :], colmask_b[:, gp, :], ALU.mult, [('CmStack', j), 'colmask_b'], [('CmTg', par)])
        pw1k = [('pw1', i) for i in range(24)]
        pw2k = [('pw2', i) for i in range(24)]
        mtk = [('MT', par, i) for i in range(24)]
        tt('dve', MT[par][:], ident_f[:].unsqueeze(1).broadcast_to([128, 24, 128]),
           PW1[:, 0:24, g:g + 1].broadcast_to([128, 24, 128]), ALU.mult, ['ident_f'] + pw1k, mtk)
        tt('pool', mtsw[:], swap_f[:].unsqueeze(1).broadcast_to([128, 24, 128]),
           PW2[:, 0:24, g:g + 1].broadcast_to([128, 24, 128]), ALU.mult, ['swap_f'] + pw2k, ['mtsw'])
        tt('dve', MT[par][:], MT[par][:], mtsw[:], ALU.add, mtk + ['mtsw'], mtk)
        for ct in range(4):
            mm(SQ(ct), BmTg[par][:], uT[:, j, ct * 512:(ct + 1) * 512], True, True,
               [('BmTg', par)] + [('uT', j, G) for G in [ct]], ['S%d' % ct])
            evac(Bb[par][0][:, PADS[0] + ct * 512:PADS[0] + (ct + 1) * 512], SQ(ct), ['S%d' % ct], [('B', par, 0, ct)])
        for s_ in range(4):
            src = Bb[par][s_]
            dst = Bb[par][s_ + 1]
            pad = PADS[s_]
            padn = PADS[s_ + 1]
            step = 8 ** s_
            for ct in range(4):
                terms = [0] + list(range(1, 8 if s_ < 3 else min(3, ct) + 1))
                for ti, m in enumerate(terms):
                    lhsT = ident_b[:] if m == 0 else MT[par][:, PWI[(s_, m)], :]
                    o0 = pad + ct * 512 - m * step
                    rk = ['ident_b'] if m == 0 else [('MT', par, PWI[(s_, m)])]
                    if s_ < 3:
                        rk += [('B', par, s_, ct)]
                        rk += [('B', par, s_, ct - 1)] if ct > 0 else [('Bpad', par, s_)]
                    else:
                        rk += [('B', par, s_, ct - m)]
                    mm(SQ(ct), lhsT, src[:, o0:o0 + 512], ti == 0, ti == len(terms) - 1, rk, ['S%d' % ct])
                evac(dst[:, padn + ct * 512:padn + (ct + 1) * 512], SQ(ct), ['S%d' % ct], [('B', par, s_ + 1, ct)])
        for ct in range(4):
            pk, pt = ('A', psA) if ct % 2 == 0 else ('B', psB)
            mm(pt[:], CmTg[par][:], Bb[par][4][:, ct * 512:(ct + 1) * 512], True, True,
               [('CmTg', par), ('B', par, 4, ct)], [pk])
            ya = yacc[:, ct * 512:(ct + 1) * 512]
            if gp == 0:
                cpy('dve', ya, pt[:], [pk], [('yacc', ct)])
            else:
                tt('dve', ya, pt[:], ya, ALU.add, [pk, ('yacc', ct)], [('yacc', ct)])
        if gp == 7:
            for ct in range(4):
                ya = yacc[:, ct * 512:(ct + 1) * 512]
                cs = slice(ct * 512, (ct + 1) * 512)
                stt(ya, uT[:, j, cs], ssmd[:, j:j + 1], ya, ALU.mult, ALU.add, [('uT', j, ct), 'ssmd', ('yacc', ct)], [('yacc', ct)])
                tt('pool', gtmp[0][:], ya, ya, ALU.mult, [('yacc', ct)], ['gtmp0'])
                ts('pool', gtmp[0][:], gtmp[0][:], 0.044715, 1.0, ALU.mult, ALU.add, ['gtmp0'], ['gtmp0'])
                tt('pool', gtmp[0][:], gtmp[0][:], ya, ALU.mult, ['gtmp0', ('yacc', ct)], ['gtmp0'])
                act(gtmp[1][:], gtmp[0][:], AF.Sigmoid, ['gtmp0'], ['gtmp1'], scale=1.5957691216057308)
                tt('dve', yg[:, j, cs], ya, gtmp[1][:], ALU.mult, [('yacc', ct), 'gtmp1'], [('yg', j, ct)])
    if STOP == 20:
        for j in range(8):
            dump("d_yg%d" % j, yg[:, j, :], [128, 2048], BF16, [('yg', j, ct) for ct in range(4)])
        return fin()

    ysT = sb("ysT", [128, 8, 2048], BF16, 112 * KB)
    w_gluS = sb("w_gluS", [128, 8, 1024], BF16, 144 * KB)
    ys_tmp = sb("ys_tmp", [128, 8, 512], F32, 160 * KB)
    sqt = [sb("sqt%d" % q, [128, 512], BF16, 176 * KB + q * 2048) for q in range(2)]
    rstdt = sb("rstdt", [128, 512], F32, 180 * KB)
    sigt = [sb("sigt%d" % q, [128, 512], F32, 182 * KB + q * 2048) for q in range(2)]
    ygk = [('yg', j, ct) for j in range(8) for ct in range(4)]
    memset('dve', st[:, 40:41], 0.0, ['phaseB_done'])
    P.add('dve', lambda e: e.memset(st[:, 41:42], 0.0), r=ygk + ['phaseB_done'], w=['phaseB_done'])
    for k in range(8):
        wdma(w_gluS[:, k, :], w_glu[k * 128:(k + 1) * 128, :], 'w_glu', ('w_glu', k), r=['phaseB_done'])
    cc = [0]
    for ct in range(4):
        cs = slice(ct * 512, (ct + 1) * 512)
        for oc in range(8):
            pk, pt = ('A', psA) if cc[0] % 2 == 0 else ('B', psB)
            q = cc[0] % 2
            cc[0] += 1
            for k in range(8):
                mm(pt[:], w_gluS[:, k, oc * 128:(oc + 1) * 128], yg[:, k, cs], k == 0, k == 7,
                   [('w_glu', k), ('yg', k, ct), 'phaseB_done'], [pk])
            act(sigt[q][:], pt[:], AF.Sigmoid, [pk, 'bglu'], [('sigt', q)], bias=bglu[:, oc:oc + 1])
            tt('dve', ys_tmp[:, oc, :], yg[:, oc, cs], sigt[q][:], ALU.mult, [('yg', oc, ct), ('sigt', q)], [('ys_tmp', oc)])
            tt('pool', sqt[q][:], ys_tmp[:, oc, :], ys_tmp[:, oc, :], ALU.mult, [('ys_tmp', oc)], [('sqt', q)])
            mm(SQ(0), ones_b[:], sqt[q][:], oc == 0, oc == 7, ['ones_b', ('sqt', q)], ['S0'])
        act(rstdt[:], SQ(0), AF.Sqrt, ['S0', 'epsc'], ['rstdt'], bias=epsc[:], scale=1.0 / 1024)
        P.add('dve', lambda e: e.reciprocal(rstdt[:], rstdt[:]), r=['rstdt'], w=['rstdt'])
        for oc in range(8):
            stt(ysT[:, oc, cs], ys_tmp[:, oc, :], sonw[:, oc:oc + 1], rstdt[:], ALU.mult, ALU.mult,
                [('ys_tmp', oc), 'sonw', 'rstdt'], [('ysT', oc, ct)])
    if STOP == 21:
        for j in range(8):
            dump("d_ysT%d" % j, ysT[:, j, :], [128, 2048], BF16, [('ysT', j, ct) for ct in range(4)])
        return fin()

    ystk = [('ysT', oc, ct) for oc in range(8) for ct in range(4)]
    memset('dve', st[:, 42:43], 0.0, ['phaseC_done'])
    P.add('dve', lambda e: e.memset(st[:, 43:44], 0.0), r=ystk + ['phaseC_done'], w=['phaseC_done'])
    KT = sb("KT", [128, 8, 2048], BF16, 20 * KB)
    Vt = sb("Vt", [128, 16, 1024], BF16, 80 * KB)
    w_ukvS = sb("w_ukvS", [128, 2, 2048], BF16, 144 * KB)
    for kk in range(2):
        wdma(w_ukvS[:, kk, :], w_ukv[kk * 128:(kk + 1) * 128, :], 'w_ukv', ('w_ukv', kk), r=['phaseC_done'])
    for G in range(4):
        for h in range(8):
            pk, pt = ('A', psA) if h % 2 == 0 else ('B', psB)
            for kk in range(2):
                mm(pt[:], w_ukvS[:, kk, h * 256:h * 256 + 128], ckvT[:, kk, G * 512:(G + 1) * 512], kk == 0, kk == 1,
                   [('w_ukv', kk), 'phaseC_done'] + [('ckvT', kk, G * 4 + jj) for jj in range(4)], [pk])
            evac(KT[:, h, G * 512:(G + 1) * 512], pt[:], [pk], [('KT', h, G)])
    for i in range(16):
        for half in range(2):
            pk, pt = ('A', psA) if half == 0 else ('B', psB)
            for hh in range(4):
                hd = half * 4 + hh
                for kk in range(2):
                    mm(pt[:, hh * 128:(hh + 1) * 128], ckvT[:, kk, i * 128:(i + 1) * 128],
                       w_ukvS[:, kk, hd * 256 + 128:hd * 256 + 256], kk == 0, kk == 1,
                       [('w_ukv', kk), ('ckvT', kk, i), 'phaseC_done'], [pk])
            evac(Vt[:, i, half * 512:(half + 1) * 512], pt[:], [pk], [('V', i, half)])

    if STOP == 23:
        return fin()
    ymT = sb("ymT", [128, 8, 2048], BF16, 144 * KB)
    w_uqS = sb("w_uqS", [128, 4, 1536], BF16, 176 * KB)
    Pb = sb("Pb", [128, 2048], BF16, 68 * KB)
    ropeT_ref[0] = sb("ropeT_D", [128, 4, 8, 32], F32, 72 * KB)
    q_tm = sb("q_tm", [128, 1600], BF16, 188 * KB)
    qTn = sb("qTn", [128, 8, 128], BF16, 188 * KB + 3200)
    qTp = sb("qTp", [128, 8, 128], BF16, 188 * KB + 5248)
    PT = sb("PT", [128, 16, 128], BF16, 188 * KB + 7296)
    o_tm = sb("o_tm", [128, 1024], F32, 188 * KB + 11392)
    on_tm = sb("on_tm", [128, 1024], BF16, 188 * KB + 15488)
    st2 = sb("st2", [128, 32], F32, 188 * KB + 17536)
    kvk = [('KT', h, G) for h in range(8) for G in range(4)] + [('V', i, hf) for i in range(16) for hf in range(2)]
    memset('dve', st[:, 44:45], 0.0, ['phaseA2_done'])
    P.add('dve', lambda e: e.memset(st[:, 45:46], 0.0), r=kvk + ['phaseA2_done'], w=['phaseA2_done'])
    for kq in range(4):
        wdma(w_uqS[:, kq, :], w_uq[kq * 128:(kq + 1) * 128, :], 'w_uq', ('w_uq', kq), r=['phaseA2_done'])
    P.add('dve', lambda e: e.memset(q_tm[:, 1536:1600], 0.0), r=['phaseA2_done'], w=['qTp_pad'])
    P.add('dve', lambda e: e.memset(kpeT[64:128, :], 0.0), r=['phaseA2_done'], w=['kpeT_pad'])
    if STOP == 30:
        return fin()
    for i in range(16):
        kend = (i + 1) * 128
        ncb = (kend + 511) // 512
        for c in range(3):
            for kq in range(4):
                mm(SQ(c), cqT[:, kq, i * 128:(i + 1) * 128], w_uqS[:, kq, c * 512:(c + 1) * 512], kq == 0, kq == 3,
                   [('cqT', kq, i), ('w_uq', kq), 'phaseA2_done'], ['S%d' % c])
        q3 = psS[:, 0:1536].rearrange("p (h d) -> p h d", h=8)
        qt3 = q_tm[:, 0:1536].rearrange("p (h d) -> p h d", h=8)
        for c in range(3):
            evac(q_tm[:, c * 512:(c + 1) * 512], SQ(c), ['S%d' % c], [('q_tm_c', c)])
        P.add('dve', lambda e: e.memset(st2[:, 31:32], 0.0), r=[('q_tm_c', c) for c in range(3)], w=['q_tm_n', 'q_tm_p'])
        if STOP == 31:
            return fin()
        rope_tm(qt3[:, :, 128:192], qt3[:, :, 128:192], i, 8, ['q_tm_p'], ['q_tm_p'])
        if STOP == 32:
            return fin()
        transpose_to(lambda k: q_tm[:, k * 192:k * 192 + 128], 8, lambda k: qTn[:, k, :], lambda k: None,
                     ['q_tm_n'], lambda k: ('qTn', k))
        if STOP == 33:
            return fin()
        KVAR = int(os.environ.get('KVAR', '0'))
        if KVAR == 1:
            transpose_to(lambda k: q_tm[:, k * 192:k * 192 + 128], 8, lambda k: qTp[:, k, :], lambda k: None,
                         ['q_tm_p', 'q_tm_n', 'qTp_pad'], lambda k: ('qTp', k))
        elif KVAR == 2:
            transpose_to(lambda k: q_tm[:, k * 192 + 128:k * 192 + 256], 4, lambda k: qTp[:, k, :], lambda k: None,
                         ['q_tm_p', 'q_tm_n', 'qTp_pad'], lambda k: ('qTp', k))
        else:
            transpose_to(lambda k: q_tm[:, k * 192 + 128:k * 192 + 256], 8, lambda k: qTp[:, k, :], lambda k: None,
                         ['q_tm_p', 'q_tm_n', 'qTp_pad'], lambda k: ('qTp', k))
        if STOP == 24:
            return fin()
        for h in range(8):
            sc = 8 + 4 * (h % 2)
            sk = ['S%d' % c for c in range(ncb)]
            for c in range(ncb):
                w_ = min(512, kend - c * 512)
                kr = [('KT', h, c)]
                mm(SQ(c)[:, 0:w_], qTn[:, h, :], KT[:, h, c * 512:c * 512 + w_], True, False,
                   [('qTn', h)] + kr + (['S0', 'S1', 'S2', 'q_tm_n'] if False else []), ['S%d' % c])
                mm(SQ(c)[:, 0:w_], qTp[:, h, :], kpeT[:, c * 512:c * 512 + w_], False, True,
                   [('qTp', h), 'qTp_pad', 'kpeT_pad'] + [('kpeT', ii) for ii in range(c * 4, min(16, c * 4 + 4))], ['S%d' % c])
            dc_, do_ = (kend - 128) // 512, (kend - 128) % 512
            tt('dve', SQ(dc_)[:, do_:do_ + 128], SQ(dc_)[:, do_:do_ + 128], cmask_f[:], ALU.add,
               ['S%d' % dc_, 'cmask_f'], ['S%d' % dc_])
            sc = 16 * (h % 2)
            for c in range(ncb):
                w_ = min(512, kend - c * 512)
                P.add('dve', lambda e, c=c, w_=w_, sc=sc: e.reduce_max(st2[:, sc + 4 + c:sc + 5 + c], SQ(c)[:, 0:w_], AX.X),
                      r=['S%d' % c], w=[('s2', sc + 4 + c)])
            P.add('dve', lambda e, sc=sc, ncb=ncb: e.reduce_max(st2[:, sc:sc + 1], st2[:, sc + 4:sc + 4 + ncb], AX.X),
                  r=[('s2', sc + 4 + c) for c in range(ncb)], w=[('s2', sc)])
            ts('dve', st2[:, sc + 1:sc + 2], st2[:, sc:sc + 1], -SCALE, None, ALU.mult, None, [('s2', sc)], [('s2', sc + 1)])
            for c in range(ncb):
                w_ = min(512, kend - c * 512)
                act(Pb[:, c * 512:c * 512 + w_], SQ(c)[:, 0:w_], AF.Exp, ['S%d' % c, ('s2', sc + 1)], ['Pb', ('s2', sc + 8 + c)],
                    bias=st2[:, sc + 1:sc + 2], scale=SCALE, accum=st2[:, sc + 8 + c:sc + 9 + c])
            P.add('dve', lambda e, sc=sc, ncb=ncb: e.reduce_sum(st2[:, sc + 2:sc + 3], st2[:, sc + 8:sc + 8 + ncb], AX.X),
                  r=[('s2', sc + 8 + c) for c in range(ncb)], w=[('s2', sc + 2)])
            P.add('dve', lambda e, sc=sc: e.reciprocal(st2[:, sc + 3:sc + 4], st2[:, sc + 2:sc + 3]),
                  r=[('s2', sc + 2)], w=[('s2', sc + 3)])
            if STOP == 25:
                return fin()
            transpose_to(lambda k: Pb[:, k * 128:(k + 1) * 128], i + 1, lambda k: PT[:, k, :], lambda k: None,
                         ['Pb'], lambda k: ('PT', k))
            pk, pt = ('A', psA) if h % 2 == 0 else ('B', psB)
            for blk in range(i + 1):
                mm(pt[:, 0:128], PT[:, blk, :], Vt[:, blk, h * 128:(h + 1) * 128], blk == 0, blk == i,
                   [('PT', blk), ('V', blk, h // 4)], [pk])
            ts('dve', o_tm[:, h * 128:(h + 1) * 128], pt[:, 0:128], st2[:, sc + 3:sc + 4], None, ALU.mult, None,
               [pk, ('s2', sc + 3)], [('o_tm', h)])
        if STOP == 26:
            return fin()
        otk = [('o_tm', h) for h in range(8)]
        act(on_tm[:], o_tm[:], AF.Square, otk, ['on_tm', 'st16'], accum=st[:, 16:17])
        rstd_from_ss(16, 1024, ['st16'], 'st16')
        ts('dve', on_tm[:], o_tm[:], st[:, 16:17], None, ALU.mult, None, otk + ['st16'], ['on_tm'])
        transpose_to(on_tm, 8, lambda k, i=i: ymT[:, k, i * 128:(i + 1) * 128], lambda k: monw[:, k:k + 1],
                     ['on_tm'], lambda k, i=i: ('ymT', k, i))
    if STOP == 22:
        allk = [('ymT', k, i) for k in range(8) for i in range(16)]
        srcs = [ysT[:, 0, :], ysT[:, 7, :], ymT[:, 0, :], ymT[:, 7, :], KT[:, 0, :], cqT[:, 0, :], kpeT[:, :], Vt[:, 0, :], Vt[:, 15, :]]
        for n_, sap in enumerate(srcs):
            wd_ = 2048 if n_ < 7 else 1024
            P.dma('pool', lambda e, n_=n_, sap=sap, wd_=wd_: e.dma_start(out=out[n_ * 128:(n_ + 1) * 128, 0:wd_], in_=sap), 'dmp',
                  r=allk, w=[('dmp', n_)])
        P.add('sp', lambda e: e.nop(), r=[('dmp', n_) for n_ in range(len(srcs))])
        return nc, P, dbg, out

    ymk = [('ymT', k, i) for k in range(8) for i in range(16)]
    memset('dve', st[:, 46:47], 0.0, ['phaseD1_done'])
    P.add('dve', lambda e: e.memset(st[:, 47:48], 0.0), r=ymk + ['phaseD1_done'], w=['phaseD1_done'])
    hT = sb("hT", [128, 4, 2048], F32, 20 * KB)
    hn2T = sb("hn2T", [128, 16, 512], BF16, 52 * KB)
    gb = [sb("gb%d" % q, [128, 4, 512], BF16, 68 * KB + q * 4096) for q in range(2)]
    wbufs = [sb("wb0", [128, 16, 512], BF16, 76 * KB), sb("wb1", [128, 16, 512], BF16, 92 * KB),
             sb("wb2", [128, 16, 512], BF16, 176 * KB)]
    hn2_tm = sb("hn2_tm", [128, 2048], BF16, 192 * KB)
    tg = [sb("tg%d" % q, [128, 512], F32, 196 * KB + q * 2048) for q in range(2)]
    sgt = sb("sgt", [128, 512], F32, 200 * KB)
    wbc = [0]

    hwc = [0]

    def next_wb(src_ap, view4=False):
        n = wbc[0] % 3
        wbc[0] += 1
        P.dma('pool', lambda e: e.dma_start(out=wbufs[n][:], in_=src_ap), 'wo%d' % n, r=['phaseD1_done'],
              w=[('wb', 2 * n), ('wb', 2 * n + 1)])
        return n

    def conv_evac(ps, pk, cidx, G, dst, dkey):
        w0, w1, w2 = convw[:, 0, cidx:cidx + 1], convw[:, 1, cidx:cidx + 1], convw[:, 2, cidx:cidx + 1]
        tk = ('tails', cidx)
        cw = [('convw', 0), ('convw', 1), ('convw', 2), 'convb']
        ts('dve', dst[:], ps, w2, convb[:, cidx:cidx + 1], ALU.mult, ALU.add, [pk] + cw, [dkey])
        stt(dst[:, 1:512], ps[:, 0:511], w1, dst[:, 1:512], ALU.mult, ALU.add, [pk, dkey] + cw, [dkey])
        stt(dst[:, 2:512], ps[:, 0:510], w0, dst[:, 2:512], ALU.mult, ALU.add, [pk, dkey] + cw, [dkey])
        if G > 0:
            stt(dst[:, 0:1], tails[:, cidx, 1:2], w1, dst[:, 0:1], ALU.mult, ALU.add, [tk, dkey] + cw, [dkey])
            stt(dst[:, 0:2], tails[:, cidx, 0:2], w0, dst[:, 0:2], ALU.mult, ALU.add, [tk, dkey] + cw, [dkey])
        if G < 3:
            cpy('dve', tails[:, cidx, 0:2], ps[:, 510:512], [pk], [tk])

    outk = []
    for G in range(4):
        for j in range(4):
            i = G * 4 + j
            P.dma('sp', lambda e, i=i, j=j: e.dma_start(out=hT[:, j, :], in_=x[i * 128:(i + 1) * 128, :]), 'h%d' % j,
                  r=['phaseD1_done'], w=[('h', j)])
        for dc in range(4):
            n = next_wb(w_out[:, dc * 512:(dc + 1) * 512].rearrange("(k p) n -> p k n", p=128))
            for j in range(4):
                i = G * 4 + j
                pk, pt = ('A', psA) if j % 2 == 0 else ('B', psB)
                for k in range(16):
                    src = ysT if k < 8 else ymT
                    rk = [('wb', 2 * n), ('wb', 2 * n + 1)]
                    rk += [('ysT', k, G)] if k < 8 else [('ymT', k - 8, i)]
                    mm(pt[:], src[:, k % 8, i * 128:(i + 1) * 128], wbufs[n][:, k, :], k == 0, k == 15,
                       rk + ['phaseD1_done'], [pk])
                hs = hT[:, j, dc * 512:(dc + 1) * 512]
                tt('dve', hs, pt[:], hs, ALU.add, [pk, ('h', j)], [('h', j)])
        for j in range(4):
            act(hn2_tm[:], hT[:, j, :], AF.Square, [('h', j)], ['hn2_tm', 'st20'], accum=st[:, 20:21])
            rstd_from_ss(20, D, ['st20'], 'st20')
            ts('dve', hn2_tm[:], hT[:, j, :], st[:, 20:21], None, ALU.mult, None, [('h', j), 'st20'], ['hn2_tm'])
            transpose_to(hn2_tm, 16, lambda k, j=j: hn2T[:, k, j * 128:(j + 1) * 128], lambda k: fnw[:, k:k + 1],
                         ['hn2_tm'], lambda k, j=j: ('hn2T', k, j))
        h2k = [('hn2T', k, j) for k in range(16) for j in range(4)]
        def hview(n, kind):
            flat = wbufs[n // 2][:].rearrange("p k c -> p (k c)")[:, (n % 2) * 4096:(n % 2 + 1) * 4096]
            if kind == 'up':
                return flat.rearrange("p (k c) -> p k c", k=16)
            return flat.rearrange("p (f c) -> p f c", f=2)

        def issue_up(hs):
            ns = []
            for src in (w_up[:, hs * 256:(hs + 1) * 256].rearrange("(k p) n -> p k n", p=128),
                        w_up[:, DFF + hs * 256:DFF + (hs + 1) * 256].rearrange("(k p) n -> p k n", p=128)):
                n = hwc[0] % 4
                hwc[0] += 1
                wdma(hview(n, 'up'), src, 'wh%d' % n, ('wb', n), r=['phaseD1_done'])
                ns.append(n)
            return ns

        def issue_dn(hs):
            n = 4 + hs % 2
            wdma(hview(n, 'dn'), w_down[hs * 256:(hs + 1) * 256, :].rearrange("(f p) n -> p f n", p=128),
                 'wh%d' % n, ('wb', n), r=['phaseD1_done'])
            return n

        def down_part(hsd, nd_, js):
            wdv_ = hview(nd_, 'dn')
            gq_ = hsd % 2
            for j in js:
                for dc in range(4):
                    pk, pt = ('A', psA) if dc % 2 == 0 else ('B', psB)
                    for ft in range(2):
                        mm(pt[:], gb[gq_][:, ft, j * 128:(j + 1) * 128], wdv_[:, ft, dc * 512:(dc + 1) * 512],
                           ft == 0, ft == 1, [('gb', gq_, ft), ('wb', nd_)], [pk])
                    hs_ = hT[:, j, dc * 512:(dc + 1) * 512]
                    tt('dve', hs_, pt[:], hs_, ALU.add, [pk, ('h', j)], [('h', j)])

        nxt = issue_up(0)
        nd_next = issue_dn(0)
        prev_nd = None
        for hs in range(23):
            if hs < 22:
                ng, nv = nxt
                nd = nd_next
                if hs < 21:
                    nxt = issue_up(hs + 1)
                gq = hs % 2
                wgv, wvv = hview(ng, 'up'), hview(nv, 'up')
            for ft in range(2):
                if hs < 22:
                    f = hs * 2 + ft
                    cg, cv = 2 * (ft % 2), 2 * (ft % 2) + 1
                    for k in range(16):
                        mm(SQ(cg), wgv[:, k, ft * 128:(ft + 1) * 128], hn2T[:, k, :], k == 0, k == 15,
                           [('wb', ng)] + (h2k if k == 0 else []), ['S%d' % cg])
                    for k in range(16):
                        mm(SQ(cv), wvv[:, k, ft * 128:(ft + 1) * 128], hn2T[:, k, :], k == 0, k == 15,
                           [('wb', nv)] + (h2k if k == 0 else []), ['S%d' % cv])
                if prev_nd is not None:
                    down_part(hs - 1, prev_nd, (2 * ft, 2 * ft + 1))
                if hs < 22:
                    conv_evac(SQ(cg), 'S%d' % cg, f, G, tg[0], 'tg0')
                    conv_evac(SQ(cv), 'S%d' % cv, 44 + f, G, tg[1], 'tg1')
                    act(sgt[:], tg[0][:], AF.Silu, ['tg0'], ['sgt'])
                    tt('pool', gb[gq][:, ft, :], sgt[:], tg[1][:], ALU.mult, ['sgt', 'tg1'], [('gb', gq, ft)])
            prev_nd = nd if hs < 22 else None
            if hs < 21:
                nd_next = issue_dn(hs + 1)
        for j in range(4):
            i = G * 4 + j
            act(hn2_tm[:], hT[:, j, :], AF.Square, [('h', j)], ['hn2_tm', 'st21'], accum=st[:, 21:22])
            rstd_from_ss(21, D, ['st21'], 'st21')
            stt(hT[:, j, :], hT[:, j, :], st[:, 21:22], fnb[:], ALU.mult, ALU.mult, [('h', j), 'st21', 'fnb'], [('h', j)])
            P.dma('sp', lambda e, i=i, j=j: e.dma_start(out=out[i * 128:(i + 1) * 128, :], in_=hT[:, j, :]), 'o%d' % j,
                  r=[('h', j)], w=[('out', i)])
            outk.append(('out', i))
    P.add('sp', lambda e: e.nop(), r=['dbgout'] + outk)
    return nc, P, dbg, out


def _consts():
    ident = np.eye(128, dtype=np.float32)
    swap = np.zeros((128, 128), np.float32)
    for q in range(128):
        swap[q, (q + 64) % 128] = 1.0
    qi = np.arange(128)[:, None]
    ki = np.arange(128)[None, :]
    cmask = np.where(ki <= qi, 0.0, -30000.0).astype(np.float32)
    colmask = np.zeros((128, 8, 128), np.float32)
    for g in range(8):
        colmask[:, g, g * 16:(g + 1) * 16] = 1.0
    grpmask = np.zeros((128, 8), np.float32)
    for g in range(8):
        grpmask[g * 16:(g + 1) * 16, g] = 1.0
    invf = (10000.0 ** (-np.arange(0, 64, 2, dtype=np.float32) / 64)).astype(np.float32)
    invf = np.broadcast_to(invf[None, :], (128, 32)).copy()
    sgn = np.ones((128, 1), np.float32)
    sgn[64:] = -1.0
    return dict(c_ident=ident, c_swap=swap, c_cmask=cmask, c_colmask=colmask.reshape(128, 1024),
                c_grpmask=grpmask, c_invf=invf, c_sgn=sgn)


def make_in_maps(inputs, cores):
    f = lambda a: np.ascontiguousarray(np.asarray(a))
    shared = dict(
        attn_norm_w=f(inputs['attn_norm_w'][0]), w_in=f(inputs['w_in'][0]),
        lam_re=f(inputs['ssm_lambda_re'][0]), lam_im=f(inputs['ssm_lambda_im'][0]),
        log_dt=f(inputs['ssm_log_dt'][0]).reshape(1, 64),
        b_re=f(inputs['ssm_b_re'][0]), b_im=f(inputs['ssm_b_im'][0]),
        c_re=f(inputs['ssm_c_re'][0]), c_im=f(inputs['ssm_c_im'][0]),
        ssm_d=f(inputs['ssm_d'][0]), w_glu=f(inputs['ssm_w_glu'][0]), b_glu=f(inputs['ssm_b_glu'][0]),
        q_norm_w=f(inputs['mla_q_norm_w'][0]), w_uq=f(inputs['mla_w_uq'][0]),
        kv_norm_w=f(inputs['mla_kv_norm_w'][0]), w_ukv=f(inputs['mla_w_ukv'][0]),
        ssm_out_norm_w=f(inputs['ssm_out_norm_w'][0]), mla_out_norm_w=f(inputs['mla_out_norm_w'][0]),
        w_out=f(inputs['w_out'][0]), ffn_norm_w=f(inputs['ffn_norm_w'][0]), w_up=f(inputs['ffn_w_up'][0]),
        conv_w=f(inputs['ffn_conv_w'][0]), conv_b=f(inputs['ffn_conv_b'][0]), w_down=f(inputs['ffn_w_down'][0]),
        final_norm_w=f(inputs['final_norm_w']).reshape(1, D),
    )
    shared.update(_consts())
    maps = []
    xs = np.asarray(inputs['x'])
    ps = np.asarray(inputs['positions'])
    for c in cores:
        m = dict(shared)
        m['x'] = f(xs[c])
        m['pos'] = f(ps[c]).reshape(L, 1).astype(np.int32)
        maps.append(m)
    return maps


def kernel(**inputs):
    nc, P, dbg, out = build_program(False)
    P.finalize()
    maps = make_in_maps(inputs, list(range(8)))
    res = run_bass_kernel_spmd(nc, maps, core_ids=list(range(8)))
    return np.stack([np.asarray(r["out"]) for r in res.results], axis=0).astype(np.float32)
```
